# Optimizing a Trainium2 kernel written in Bass

```python
import math
import jax, jax.numpy as jnp
from jax import lax
import numpy as np

D_MODEL = 4096
BATCH = 1
SEQ = 8192
DEPTH = 1

CHUNK = 64
Q_BLOCK = 128
ROPE_THETA = 10000.0
EPS = 1e-6
MIX_WIDTH = D_MODEL
MLA_HEADS = 16
MLA_Q_LORA = 768
MLA_KV_LORA = 512
MLA_NOPE = 128
MLA_ROPE = 64
MLA_QK = MLA_NOPE + MLA_ROPE
MLA_V = 128
MLA_SCALE = 1.0 / math.sqrt(MLA_QK)
DIFF_HEADS = 8
DIFF_HEAD_DIM = 128
DIFF_V = 2 * DIFF_HEAD_DIM
DIFF_SCALE = 1.0 / math.sqrt(DIFF_HEAD_DIM)
DIFF_QK_COLS = 2 * DIFF_HEADS * DIFF_HEAD_DIM
DIFF_V_COLS = DIFF_HEADS * DIFF_V
IN_SIZES = (MLA_Q_LORA, MLA_KV_LORA, MLA_ROPE, DIFF_QK_COLS, DIFF_QK_COLS, DIFF_V_COLS)
D_IN = MLA_Q_LORA + MLA_KV_LORA + MLA_ROPE + 2 * DIFF_QK_COLS + DIFF_V_COLS
D_FF = -(-8 * D_MODEL // (3 * 256)) * 256

kernel_name = "hymba_mla_diffattn_swiglu_chunk_causal"


def _rms_norm(t, g):
    tf = t.astype(jnp.float32)
    y = tf * lax.rsqrt(jnp.mean(tf * tf, axis=-1, keepdims=True) + EPS)
    return (y * g.astype(jnp.float32)).astype(t.dtype)


def _rope(t, pos):
    half = t.shape[-1] // 2
    inv_freq = ROPE_THETA ** (-jnp.arange(half, dtype=jnp.float32) / half)
    ang = pos.astype(jnp.float32)[:, None] * inv_freq[None, :]
    cos = jnp.cos(ang)[:, None, :]
    sin = jnp.sin(ang)[:, None, :]
    tf = t.astype(jnp.float32)
    t1, t2 = tf[..., :half], tf[..., half:]
    return jnp.concatenate([t1 * cos - t2 * sin, t2 * cos + t1 * sin], axis=-1).astype(t.dtype)


def _to_blocks(t):
    b, s = t.shape[:2]
    return jnp.moveaxis(t.reshape((b, s // Q_BLOCK, Q_BLOCK) + t.shape[2:]), 1, 0)


def _from_blocks(o):
    o = jnp.moveaxis(o, 0, 1)
    return o.reshape((o.shape[0], o.shape[1] * o.shape[2]) + o.shape[3:])


def _chunk_softmax(s, blk):
    q_pos = blk * Q_BLOCK + jnp.arange(Q_BLOCK)
    k_pos = jnp.arange(s.shape[-1])
    allowed = (k_pos[None, :] // CHUNK) <= (q_pos[:, None] // CHUNK)
    s = jnp.where(allowed, s, jnp.finfo(jnp.float32).min)
    return jax.nn.softmax(s, axis=-1)


def _mla_attend(q, k, v):
    def body(args):
        qb, blk = args
        s = jnp.einsum('bqhd,bkhd->bhqk', qb, k, preferred_element_type=jnp.float32) * MLA_SCALE
        p = _chunk_softmax(s, blk)
        return jnp.einsum('bhqk,bkhd->bqhd', p.astype(v.dtype), v)
    nb = q.shape[1] // Q_BLOCK
    return _from_blocks(lax.map(body, (_to_blocks(q), jnp.arange(nb))))


def _diff_attend(q1, q2, k1, k2, v, lam):
    def body(args):
        qb1, qb2, blk = args
        s1 = jnp.einsum('bqhd,bkhd->bhqk', qb1, k1, preferred_element_type=jnp.float32) * DIFF_SCALE
        s2 = jnp.einsum('bqhd,bkhd->bhqk', qb2, k2, preferred_element_type=jnp.float32) * DIFF_SCALE
        p = _chunk_softmax(s1, blk) - lam * _chunk_softmax(s2, blk)
        return jnp.einsum('bhqk,bkhd->bqhd', p.astype(v.dtype), v)
    nb = q1.shape[1] // Q_BLOCK
    return _from_blocks(lax.map(body, (_to_blocks(q1), _to_blocks(q2), jnp.arange(nb))))


def setup_inputs(seed: int = 0) -> dict:
    key = jax.random.key(seed)
    ks = jax.random.split(key, 22)
    f32 = jnp.float32

    def w(k, fan_in, fan_out):
        return jax.random.normal(k, (DEPTH, fan_in, fan_out), f32) * fan_in ** -0.5

    def gain(k, n):
        return 1.0 + 0.02 * jax.random.normal(k, (DEPTH, n), f32)

    def lam_vec(k):
        return 0.1 * jax.random.normal(k, (DEPTH, DIFF_HEAD_DIM), f32)

    return {
        "x": jax.random.normal(ks[0], (BATCH, SEQ, D_MODEL), f32),
        "attn_norm_g": gain(ks[1], D_MODEL),
        "w_in": w(ks[2], D_MODEL, D_IN),
        "q_latent_norm_g": gain(ks[3], MLA_Q_LORA),
        "kv_latent_norm_g": gain(ks[4], MLA_KV_LORA),
        "w_uq": w(ks[5], MLA_Q_LORA, MLA_HEADS * MLA_QK),
        "w_ukv": w(ks[6], MLA_KV_LORA, MLA_HEADS * (MLA_NOPE + MLA_V)),
        "mla_q_norm_g": gain(ks[7], MLA_QK),
        "mla_k_norm_g": gain(ks[8], MLA_QK),
        "diff_q_norm_g": gain(ks[9], DIFF_HEAD_DIM),
        "diff_k_norm_g": gain(ks[10], DIFF_HEAD_DIM),
        "lambda_q1": lam_vec(ks[11]),
        "lambda_k1": lam_vec(ks[12]),
        "lambda_q2": lam_vec(ks[13]),
        "lambda_k2": lam_vec(ks[14]),
        "diff_subln_g": gain(ks[15], DIFF_V),
        "w_o": w(ks[16], MIX_WIDTH, D_MODEL),
        "ffn_norm_g": gain(ks[17], D_MODEL),
        "w_gate": w(ks[18], D_MODEL, D_FF),
        "w_up": w(ks[19], D_MODEL, D_FF),
        "w_down": w(ks[20], D_FF, D_MODEL),
    }


def reference(x, attn_norm_g, w_in, q_latent_norm_g, kv_latent_norm_g, w_uq, w_ukv,
              mla_q_norm_g, mla_k_norm_g, diff_q_norm_g, diff_k_norm_g,
              lambda_q1, lambda_k1, lambda_q2, lambda_k2, diff_subln_g, w_o,
              ffn_norm_g, w_gate, w_up, w_down):
    b, s, _ = x.shape
    pos = jnp.arange(s)
    split_idx = list(np.cumsum(IN_SIZES)[:-1])
    h = x
    for layer in range(DEPTH):
        lambda_init = 0.8 - 0.6 * math.exp(-0.3 * layer)
        n = _rms_norm(h, attn_norm_g[layer])
        z = n @ w_in[layer]
        c_q, c_kv, k_pe, dq, dk, dv = jnp.split(z, split_idx, axis=-1)

        q = (_rms_norm(c_q, q_latent_norm_g[layer]) @ w_uq[layer]).reshape(b, s, MLA_HEADS, MLA_QK)
        kv = (_rms_norm(c_kv, kv_latent_norm_g[layer]) @ w_ukv[layer]).reshape(b, s, MLA_HEADS, MLA_NOPE + MLA_V)
        k_nope, v_a = kv[..., :MLA_NOPE], kv[..., MLA_NOPE:]
        k_pe = jnp.broadcast_to(k_pe[:, :, None, :], (b, s, MLA_HEADS, MLA_ROPE))
        k = jnp.concatenate([k_nope, k_pe], axis=-1)
        q = _rms_norm(q, mla_q_norm_g[layer])
        k = _rms_norm(k, mla_k_norm_g[layer])
        q = jnp.concatenate([q[..., :MLA_NOPE], _rope(q[..., MLA_NOPE:], pos)], axis=-1)
        k = jnp.concatenate([k[..., :MLA_NOPE], _rope(k[..., MLA_NOPE:], pos)], axis=-1)
        o_a = _mla_attend(q, k, v_a).reshape(b, s, MLA_HEADS * MLA_V)

        dq = _rope(_rms_norm(dq.reshape(b, s, 2 * DIFF_HEADS, DIFF_HEAD_DIM), diff_q_norm_g[layer]), pos)
        dk = _rope(_rms_norm(dk.reshape(b, s, 2 * DIFF_HEADS, DIFF_HEAD_DIM), diff_k_norm_g[layer]), pos)
        dq = dq.reshape(b, s, DIFF_HEADS, 2, DIFF_HEAD_DIM)
        dk = dk.reshape(b, s, DIFF_HEADS, 2, DIFF_HEAD_DIM)
        dv = dv.reshape(b, s, DIFF_HEADS, DIFF_V)
        lam = (jnp.exp(jnp.sum(lambda_q1[layer].astype(jnp.float32) * lambda_k1[layer].astype(jnp.float32)))
               - jnp.exp(jnp.sum(lambda_q2[layer].astype(jnp.float32) * lambda_k2[layer].astype(jnp.float32)))
               + lambda_init)
        o_b = _diff_attend(dq[..., 0, :], dq[..., 1, :], dk[..., 0, :], dk[..., 1, :], dv, lam)
        o_b = (_rms_norm(o_b, diff_subln_g[layer]) * (1.0 - lambda_init)).reshape(b, s, DIFF_HEADS * DIFF_V)

        h = h + jnp.concatenate([o_a, o_b], axis=-1) @ w_o[layer]

        m = _rms_norm(h, ffn_norm_g[layer])
        h = h + (jax.nn.silu(m @ w_gate[layer]) * (m @ w_up[layer])) @ w_down[layer]
    return h
```

```python
import math
from contextlib import ExitStack
import numpy as np
import ml_dtypes
import concourse.bass as bass
import concourse.mybir as mybir
from concourse.bass_utils import run_bass_kernel_spmd

F32 = mybir.dt.float32
BF16 = mybir.dt.bfloat16
ALU = mybir.AluOpType
AF = mybir.ActivationFunctionType
AX = mybir.AxisListType

NCORES = 8
S = 8192
D = 4096
TOWN = 1024
EPS = 1e-6
QL, KVL, ROPE = 768, 512, 64
NH, QK, NOPE, VD = 16, 192, 128, 128
DH, DHD, DV = 8, 128, 256
D_IN = 7488
DFF = 11008
C_CQ, C_CKV, C_KPE, C_DQ, C_DK, C_DV = 0, 768, 1280, 1344, 3392, 5440
MLA_SCALE = 1.0 / math.sqrt(QK)
DIFF_SCALE = 1.0 / math.sqrt(DHD)
LAMBDA_INIT = 0.8 - 0.6 * math.exp(0.0)
DEBUG = False
_UID = [0]


class Buf:
    def __init__(self, name, persistent=False):
        self.name = name
        self.lw = {}
        self.lr = {}
        self.dsem = None
        self.persistent = persistent


class T:
    def __init__(self, h, name, persistent=False):
        self.h = h
        self.buf = Buf(name, persistent)

    def __getitem__(self, idx):
        return self.h[idx]


class Op:
    __slots__ = ("eng", "fn", "deps", "signal", "cidx", "dsem", "dval", "kind")


class _Rec:
    def __init__(self):
        self.call = None

    def __getattr__(self, name):
        def f(*a, **k):
            self.call = (name, a, k)
            return self
        return f


def _freeze(fn):
    r = _Rec()
    fn(r)
    name, a, k = r.call
    return lambda e: getattr(e, name)(*a, **k)


class Prog:
    COMPUTE = ("pe", "act", "dve")
    ALL = ("pe", "act", "dve", "pool", "sp")

    def __init__(self, nc):
        self.nc = nc
        self.csem = {e: nc.alloc_semaphore(name=f"c_{e}") for e in self.COMPUTE}
        self.cbase = {e: 0 for e in self.COMPUTE}
        self.dsems = [nc.alloc_semaphore(name=f"d_{i}") for i in range(94)]
        self.dcount = [0] * len(self.dsems)
        self.dfree = list(range(len(self.dsems)))
        self.reset_phase()
        self.persist = []

    def reset_phase(self):
        self.streams = {e: [] for e in self.ALL}
        self.ncomp = {e: 0 for e in self.COMPUTE}
        self.phase_bufs = []

    def _deps(self, op, reads, writes, nowaw):
        deps = {}

        def add(d):
            for k, v in d.items():
                if deps.get(k, -1) < v:
                    deps[k] = v

        for t in reads:
            add(t.buf.lw)
        for t in writes:
            add(t.buf.lr)
            if t not in nowaw:
                add(t.buf.lw)
        if op.eng == "pe":
            deps.pop("pe", None)
        op.deps = deps

    def _commit(self, clock, val, reads, writes, nowaw):
        for t in reads:
            b = t.buf
            if b.lr.get(clock, -1) < val:
                b.lr[clock] = val
        for t in writes:
            b = t.buf
            if t in nowaw:
                if b.lw.get(clock, -1) < val:
                    b.lw[clock] = val
            else:
                b.lw = {clock: val}
                b.lr = {}

    def comp(self, eng, fn, reads=(), writes=(), signal=True, nowaw=()):
        op = Op()
        op.eng, op.fn, op.signal, op.kind = eng, _freeze(fn), signal, "c"
        self._deps(op, reads, writes, nowaw)
        op.cidx = self.ncomp[eng]
        self.ncomp[eng] += 1
        self._commit(eng, op.cidx, reads, writes, nowaw)
        self.streams[eng].append(op)
        return op

    def dma(self, queue, out_ap, in_ap, reads=(), writes=(), nowaw=()):
        assert len(writes) == 1
        dst = writes[0].buf
        if dst.dsem is None:
            dst.dsem = self.dfree.pop(0)
            if not dst.persistent:
                self.phase_bufs.append(dst)
        op = Op()
        op.eng, op.kind, op.signal = queue, "d", True
        op.fn = lambda e: e.dma_start(out=out_ap, in_=in_ap)
        self._deps(op, reads, writes, nowaw)
        self.dcount[dst.dsem] += 16
        op.dsem, op.dval = dst.dsem, self.dcount[dst.dsem]
        self._commit(("d", dst.dsem), op.dval, reads, writes, nowaw)
        self.streams[queue].append(op)
        return op

    def emit(self, all_ts):
        nc = self.nc
        sigval = {}
        for e in self.COMPUTE:
            ops = self.streams[e]
            if ops:
                ops[-1].signal = True
            vals = [0] * len(ops)
            cnt = self.cbase[e]
            for i, op in enumerate(ops):
                if op.signal:
                    cnt += 1
                vals[i] = cnt
            nxt = None
            res = [0] * len(ops)
            for i in range(len(ops) - 1, -1, -1):
                if ops[i].signal:
                    nxt = vals[i]
                res[i] = nxt
            sigval[e] = res
            self.cbase_next = getattr(self, "cbase_next", {})
            self.cbase_next[e] = cnt
        final = []
        for e in self.COMPUTE:
            if self.streams[e]:
                final.append((self.csem[e], self.cbase_next[e]))
        used_d = set()
        for q in ("pool", "sp"):
            for op in self.streams[q]:
                used_d.add(op.dsem)
        for dsi in used_d:
            final.append((self.dsems[dsi], self.dcount[dsi]))

        def run_stream(ename):
            ops = self.streams[ename]

            def body(eng):
                waited = {}

                def wait(sem, val):
                    key = id(sem)
                    if waited.get(key, -1) >= val:
                        return
                    waited[key] = val
                    eng.wait_ge(sem, val)

                for op in ops:
                    for clock, v in op.deps.items():
                        if isinstance(clock, tuple):
                            wait(self.dsems[clock[1]], v)
                        else:
                            wait(self.csem[clock], sigval[clock][v])
                    ins = op.fn(eng)
                    if op.kind == "d":
                        ins.then_inc(self.dsems[op.dsem], 16)
                    elif op.signal:
                        ins.then_inc(self.csem[op.eng], 1)
                for sem, val in final:
                    wait(sem, val)

            return body

        with nc.Block() as block:
            block.tensor(run_stream("pe"))
            block.scalar(run_stream("act"))
            block.vector(run_stream("dve"))
            block.gpsimd(run_stream("pool"))
            block.sync(run_stream("sp"))
        for e in self.COMPUTE:
            self.cbase[e] = self.cbase_next[e]
        for b in self.phase_bufs:
            self.dfree.append(b.dsem)
            b.dsem = None
        for t in all_ts:
            t.buf.lw = {}
            t.buf.lr = {}
        self.reset_phase()


def build_program():
    nc = bass.Bass("TRN2", target_bir_lowering=False)
    P = Prog(nc)
    persist_ts = []

    def dram(name, shape, dt, kind="Internal"):
        if DEBUG and kind == "Internal" and name in DEBUG_OUT:
            kind = "ExternalOutput"
        t = T(nc.dram_tensor(name, shape, dt, kind=kind).ap(), name, persistent=True)
        persist_ts.append(t)
        return t

    x = dram("x", [S, D], F32, "ExternalInput")
    xo = dram("xo", [TOWN, D], F32, "ExternalInput")
    w_in = dram("w_in", [D, D_IN], F32, "ExternalInput")
    w_uq = dram("w_uq", [QL, NH * QK], F32, "ExternalInput")
    w_ukv = dram("w_ukv", [KVL, NH * (NOPE + VD)], F32, "ExternalInput")
    w_o = dram("w_o", [D, D], F32, "ExternalInput")
    w_gate = dram("w_gate", [D, DFF], F32, "ExternalInput")
    w_up = dram("w_up", [D, DFF], F32, "ExternalInput")
    w_down = dram("w_down", [DFF, D], F32, "ExternalInput")
    gb = dram("gb", [2, 128, D], F32, "ExternalInput")
    pc = dram("pc", [128, 32], F32, "ExternalInput")
    lamv = dram("lamv", [128, 512], F32, "ExternalInput")
    cmat = dram("cmat", [4, 128, 128], BF16, "ExternalInput")
    tk = dram("tk", [128, 4, S], F32, "ExternalInput")
    tq = dram("tq", [128, 4, TOWN], F32, "ExternalInput")
    msk = dram("msk", [128, 8, 128], BF16, "ExternalInput")
    out = dram("out", [TOWN, D], F32, "ExternalOutput")
    nT = dram("nT", [32, 128, S + TOWN], BF16)
    sKn = dram("sKn", [NH, 128, S], BF16)
    sKr = dram("sKr", [NH, 64, S], BF16)
    sVm = dram("sVm", [NH, 128, 64, VD], BF16)
    sDK = dram("sDK", [2 * DH, 128, S], BF16)
    sDV = dram("sDV", [DH, 128, 64, DV], BF16)
    sQn = dram("sQn", [NH, 128, TOWN], BF16)
    sQr = dram("sQr", [NH, 64, TOWN], BF16)
    sDQ = dram("sDQ", [2 * DH, 128, TOWN], BF16)
    hbuf = dram("hbuf", [TOWN, D], F32)

    class Phase:
        def __init__(self):
            self.es = ExitStack()
            self.ts = []
            self.n = 0

        def sb(self, name, shape, dt):
            _UID[0] += 1
            h = self.es.enter_context(nc.sbuf_tensor(f"{name}_{_UID[0]}", shape, dt))
            t = T(h, name)
            self.ts.append(t)
            return t

        def ps(self, name, shape, dt):
            _UID[0] += 1
            h = self.es.enter_context(nc.psum_tensor(f"{name}_{_UID[0]}", shape, dt))
            t = T(h, name)
            self.ts.append(t)
            return t

        def finish(self, extra=()):
            P.emit(self.ts + list(extra) + persist_ts)
            self.es.close()

    class Deferred:
        def __init__(self, depth):
            self.depth = depth
            self.q = []

        def push(self, fn):
            self.q.append(fn)
            while len(self.q) > self.depth:
                self.q.pop(0)()

        def flush(self):
            while self.q:
                self.q.pop(0)()

    class Ring:
        def __init__(self, items):
            self.items = items
            self.i = 0

        def next(self):
            t = self.items[self.i % len(self.items)]
            self.i += 1
            return t

    def load_consts(ph):
        cm = ph.sb("cm", [128, 4, 128], BF16)
        P.dma("pool", cm[:], cmat[:].rearrange("a p q -> p a q"), reads=[cmat], writes=[cm])
        pcs = ph.sb("pcs", [128, 32], F32)
        P.dma("pool", pcs[:], pc[:], reads=[pc], writes=[pcs])
        return cm, pcs

    def norm_rows(src, ntok, gidx, col0):
        ph = Phase()
        cm, pcs = load_consts(ph)
        gbt = ph.sb("gbt", [128, D], F32)
        P.dma("pool", gbt[:], gb[gidx], reads=[gb], writes=[gbt])
        xt = Ring([ph.sb("xt", [128, D], F32) for _ in range(3)])
        nt = Ring([ph.sb("nt", [128, D], BF16) for _ in range(3)])
        junk = ph.sb("junk", [128, D], BF16)
        ssr = Ring([ph.sb("ss", [128, 1], F32) for _ in range(3)])
        rmr = Ring([ph.sb("rm", [128, 1], F32) for _ in range(3)])
        rsr = Ring([ph.sb("rs", [128, 1], F32) for _ in range(3)])
        stg = Ring([ph.sb("stg", [128, 32, 512], BF16) for _ in range(2)])
        pst = Ring([ph.ps("pst", [128, 1024], BF16) for _ in range(4)])
        ident = cm[:, 0, :]
        ntiles = ntok // 128
        ntts = {}

        def stage_a(ti):
            r0 = ti * 128
            xtt, ntt, ss, rm, rs = xt.next(), nt.next(), ssr.next(), rmr.next(), rsr.next()
            P.dma("pool", xtt[:], src[r0:r0 + 128, :], reads=[src], writes=[xtt])
            P.comp("dve", lambda e: e.memset(ss[:], 0.0), writes=[ss])
            P.comp("act", lambda e: e.activation(out=junk[:], in_=xtt[:], func=AF.Square, accum_out=ss[:]),
                   reads=[xtt, ss], writes=[junk, ss])
            P.comp("act", lambda e: e.activation(out=rm[:], in_=ss[:], func=AF.Sqrt, scale=1.0 / D, bias=EPS),
                   reads=[ss], writes=[rm])
            P.comp("dve", lambda e: e.reciprocal(out=rs[:], in_=rm[:]), reads=[rm], writes=[rs])
            P.comp("dve", lambda e: e.scalar_tensor_tensor(
                out=ntt[:], in0=xtt[:], scalar=rs[:, 0:1], in1=gbt[:], op0=ALU.mult, op1=ALU.mult),
                reads=[xtt, rs, gbt], writes=[ntt])
            ntts[ti] = ntt

        sts = {}

        def stage_b(ti):
            g, t = ti // 4, ti % 4
            if t == 0:
                sts[g] = stg.next()
            st = sts[g]
            ntt = ntts.pop(ti)
            for q in range(4):
                pb = pst.next()
                for j in range(8):
                    kc = q * 8 + j
                    P.comp("pe", lambda e: e.transpose(
                        out=pb[:, j * 128:(j + 1) * 128], in_=ntt[:, kc * 128:(kc + 1) * 128], identity=ident),
                        reads=[ntt, cm], writes=[pb], signal=(j == 7))
                dst = st[:, q * 8:(q + 1) * 8, t * 128:(t + 1) * 128]
                srcv = pb[:].rearrange("p (a b) -> p a b", a=8)
                if q % 2 == 0:
                    P.comp("act", lambda e: e.copy(out=dst, in_=srcv), reads=[pb], writes=[st], nowaw=[st])
                else:
                    P.comp("dve", lambda e: e.tensor_copy(out=dst, in_=srcv), reads=[pb], writes=[st], nowaw=[st])
            if t == 3:
                c0 = col0 + g * 512
                P.dma("sp", nT[:, :, c0:c0 + 512].rearrange("k p t -> p k t"), st[:], reads=[st], writes=[nT], nowaw=[nT])

        stage_a(0)
        for ti in range(ntiles):
            if ti + 1 < ntiles:
                stage_a(ti + 1)
            stage_b(ti)
        ph.finish()

    def make_p1_common(ph, cm, pcs, nring=3):
        C = {}
        C["ring"] = Ring([ph.sb("wr", [128, 8192], BF16) for _ in range(nring)])
        C["psr"] = Ring([ph.ps("psb", [128, 512], F32) for _ in range(8)])
        C["psA"] = Ring(C["psr"].items[0:5])
        C["psB"] = Ring(C["psr"].items[5:8])
        C["sq"] = Ring([ph.sb("sq", [128, 512], BF16) for _ in range(5)])
        C["xg"] = Ring([ph.sb("xg", [128, 512], BF16) for _ in range(4)])
        C["rm"] = Ring([ph.sb("rmb", [128, 512], F32) for _ in range(2)])
        C["rs"] = Ring([ph.sb("rsb", [128, 512], F32) for _ in range(3)])
        C["t1"] = Ring([ph.sb("t1", [128, 512], F32) for _ in range(2)])
        C["t2"] = Ring([ph.sb("t2", [128, 512], F32) for _ in range(2)])
        C["ob"] = Ring([ph.sb("ob", [128, 512], BF16) for _ in range(4)])
        return C

    def load_w(C, wsrc, r0, nkc, c0, ncols):
        slot = C["ring"].next()
        v = slot[:, 0:nkc * ncols].rearrange("p (k n) -> p k n", k=nkc)
        P.dma("pool", v, wsrc[r0:r0 + nkc * 128, c0:c0 + ncols].rearrange("(k p) n -> p k n", p=128),
              reads=[wsrc], writes=[slot])
        return slot, v

    def gemm_fm(C, slot, wv, mcols, m0, act, abufs, nkc, tok, psring=None):
        pb = (psring or C["psr"]).next()
        for kc in range(nkc):
            P.comp("pe", lambda e, pb=pb, kc=kc: e.matmul(
                pb[0:mcols, :], lhsT=wv[:, kc, m0:m0 + mcols], rhs=act(kc, tok),
                start=(kc == 0), stop=(kc == nkc - 1)),
                reads=[slot] + abufs, writes=[pb], signal=(kc == nkc - 1))
        return pb

    def square(C, pb, rows):
        sq = C["sq"].next()
        P.comp("act", lambda e: e.activation(out=sq[0:rows, :], in_=pb[0:rows, :], func=AF.Square),
               reads=[pb], writes=[sq])
        return sq

    def scaled_copy(C, pb, rows, gcol, pcs):
        xg = C["xg"].next()
        P.comp("act", lambda e: e.activation(out=xg[0:rows, :], in_=pb[0:rows, :], func=AF.Copy,
                                             scale=pcs[0:rows, gcol:gcol + 1]),
               reads=[pb, pcs], writes=[xg])
        return xg

    def ss_rstd(C, cm, terms, dnorm, psring=None):
        pb = (psring or C["psr"]).next()
        for i, (sq, rows) in enumerate(terms):
            P.comp("pe", lambda e, sq=sq, rows=rows, i=i: e.matmul(
                pb[:, :], lhsT=cm[0:rows, 1, :], rhs=sq[0:rows, :],
                start=(i == 0), stop=(i == len(terms) - 1)),
                reads=[sq, cm], writes=[pb], signal=(i == len(terms) - 1))
        rm = C["rm"].next()
        P.comp("act", lambda e: e.activation(out=rm[:], in_=pb[:], func=AF.Sqrt, scale=1.0 / dnorm, bias=EPS),
               reads=[pb], writes=[rm])
        rs = C["rs"].next()
        P.comp("dve", lambda e: e.reciprocal(out=rs[:], in_=rm[:]), reads=[rm], writes=[rs])
        return rs

    def rot_mm(C, cm, xg, rows):
        pb = C["psr"].next()
        ri = 2 if rows == 128 else 3
        P.comp("pe", lambda e: e.matmul(pb[0:rows, :], lhsT=cm[0:rows, ri, 0:rows], rhs=xg[0:rows, :],
                                        start=True, stop=True), reads=[xg, cm], writes=[pb])
        return pb

    def rope_combine(C, xg, prot, rows, cosap, sinap, tabbuf):
        t1 = C["t1"].next()
        t2 = C["t2"].next()
        P.comp("dve", lambda e: e.tensor_tensor(out=t1[0:rows, :], in0=xg[0:rows, :], in1=cosap, op=ALU.mult),
               reads=[xg, tabbuf], writes=[t1])
        P.comp("dve", lambda e: e.tensor_tensor(out=t2[0:rows, :], in0=prot[0:rows, :], in1=sinap, op=ALU.mult),
               reads=[prot, tabbuf], writes=[t2])
        P.comp("dve", lambda e: e.tensor_tensor(out=t1[0:rows, :], in0=t1[0:rows, :], in1=t2[0:rows, :], op=ALU.add),
               reads=[t1, t2], writes=[t1])
        return t1

    def finish_mul(C, src, rows, rs, dst_dram, dst_ap, extra_reads=()):
        ob = C["ob"].next()
        P.comp("dve", lambda e: e.tensor_tensor(out=ob[0:rows, :], in0=src[0:rows, :], in1=rs[0:rows, :], op=ALU.mult),
               reads=[src, rs] + list(extra_reads), writes=[ob])
        P.dma("sp", dst_ap, ob[0:rows, :], reads=[ob], writes=[dst_dram], nowaw=[dst_dram])

    def finish_plain(C, pb, rows, gcol, pcs, rs, dst_dram, dst_ap):
        ob = C["ob"].next()
        P.comp("dve", lambda e: e.scalar_tensor_tensor(
            out=ob[0:rows, :], in0=pb[0:rows, :], scalar=pcs[0:rows, gcol:gcol + 1], in1=rs[0:rows, :],
            op0=ALU.mult, op1=ALU.mult), reads=[pb, pcs, rs], writes=[ob])
        P.dma("sp", dst_ap, ob[0:rows, :], reads=[ob], writes=[dst_dram], nowaw=[dst_dram])

    def latent_norm(C, cm, pcs, wsrc, ccol0, nch, gcol0, dnorm, nTp, raw, outn, half):
        tok = slice(half * 512, (half + 1) * 512)
        sqs = []
        for g2 in range(nch // 2):
            slot, wv = load_w(C, wsrc, 0, 32, ccol0 + g2 * 256, 256)
            for j in range(2):
                ch = g2 * 2 + j
                pb = gemm_fm(C, slot, wv, 128, j * 128, lambda kc, tok: nTp[:, kc, tok], [nTp], 32, tok)
                sq = square(C, pb, 128)
                P.comp("act", lambda e, pb=pb, ch=ch: e.copy(out=raw[:, ch, :], in_=pb[:, :]),
                       reads=[pb], writes=[raw], nowaw=[raw])
                sqs.append(sq)
                if len(sqs) == 3 or ch == nch - 1:
                    pass
        return sqs

    def phase1a():
        ph = Phase()
        cm, pcs = load_consts(ph)
        C = make_p1_common(ph, cm, pcs, nring=3)
        nTp = ph.sb("nTp", [128, 32, 1024], BF16)
        tabs = [ph.sb("tab", [128, 4, 512], F32) for _ in range(2)]
        ckraw = [ph.sb("ckraw", [128, 4, 512], F32) for _ in range(2)]
        cksq = [[ph.sb("cksq", [128, 512], BF16) for _ in range(4)] for _ in range(2)]
        ckvn = ph.sb("ckvn", [128, 4, 1024], BF16)
        kpsq = [ph.sb("kpsq", [128, 512], BF16) for _ in range(2)]
        for t_ in kpsq:
            P.comp("dve", lambda e: e.memset(t_[64:128, :], 0.0), writes=[t_])
        kprot = [ph.sb("kprot", [64, 512], F32) for _ in range(2)]
        vst = Ring([ph.sb("vst", [128, 1024], BF16) for _ in range(2)])
        dvst = Ring([ph.sb("dvst", [128, 256], BF16) for _ in range(3)])
        act_n = lambda kc, tok: nTp[:, kc, tok]
        toks = [slice(0, 512), slice(512, 1024)]
        for p in range(S // 1024):
            t0 = p * 1024
            P.dma("pool", nTp[:], nT[:, :, t0:t0 + 1024].rearrange("k p t -> p k t"), reads=[nT], writes=[nTp])
            for half in range(2):
                g0 = t0 + half * 512
                P.dma("pool", tabs[half][:], tk[:, :, g0:g0 + 512], reads=[tk], writes=[tabs[half]])
            for g2 in range(2):
                slot, wv = load_w(C, w_in, 0, 32, C_CKV + g2 * 256, 256)
                for half in range(2):
                    for j in range(2):
                        ch = g2 * 2 + j
                        pb = gemm_fm(C, slot, wv, 128, j * 128, act_n, [nTp], 32, toks[half])
                        sqd = cksq[half][ch]
                        P.comp("act", lambda e: e.activation(out=sqd[:], in_=pb[:], func=AF.Square),
                               reads=[pb], writes=[sqd])
                        P.comp("act", lambda e: e.copy(out=ckraw[half][:, ch, :], in_=pb[:, :]),
                               reads=[pb], writes=[ckraw[half]], nowaw=[ckraw[half]])
            for half in range(2):
                rs = ss_rstd(C, cm, [(cksq[half][ch], 128) for ch in range(4)], KVL)
                for ch in range(4):
                    P.comp("dve", lambda e: e.scalar_tensor_tensor(
                        out=ckvn[:, ch, toks[half]], in0=ckraw[half][:, ch, :], scalar=pcs[:, 6 + ch:7 + ch], in1=rs[:],
                        op0=ALU.mult, op1=ALU.mult), reads=[ckraw[half], pcs, rs], writes=[ckvn], nowaw=[ckvn])
            slot, wv = load_w(C, w_in, 0, 32, C_KPE, 64)
            for half in range(2):
                tab = tabs[half]
                pb = gemm_fm(C, slot, wv, 64, 0, act_n, [nTp], 32, toks[half])
                P.comp("act", lambda e: e.activation(out=kpsq[half][0:64, :], in_=pb[0:64, :], func=AF.Square),
                       reads=[pb], writes=[kpsq[half]], nowaw=[kpsq[half]])
                xg = scaled_copy(C, pb, 64, 13, pcs)
                prot = rot_mm(C, cm, xg, 64)
                t3 = rope_combine(C, xg, prot, 64, tab[0:64, 2, :], tab[0:64, 3, :], tab)
                P.comp("dve", lambda e: e.tensor_copy(out=kprot[half][:], in_=t3[0:64, :]), reads=[t3], writes=[kprot[half]])
            dfd = Deferred(1)
            for g2 in range(8):
                slot, wv = load_w(C, w_in, 0, 32, C_DK + g2 * 256, 256)
                for half in range(2):
                    tab = tabs[half]
                    g0 = t0 + half * 512
                    for j in range(2):
                        hm = g2 * 2 + j
                        pb = gemm_fm(C, slot, wv, 128, j * 128, act_n, [nTp], 32, toks[half])
                        sq = square(C, pb, 128)
                        xg = scaled_copy(C, pb, 128, 15, pcs)

                        def st2(sq=sq, xg=xg, tab=tab, hm=hm, g0=g0):
                            rs = ss_rstd(C, cm, [(sq, 128)], DHD)
                            prot = rot_mm(C, cm, xg, 128)
                            t3 = rope_combine(C, xg, prot, 128, tab[:, 0, :], tab[:, 1, :], tab)
                            finish_mul(C, t3, 128, rs, sDK, sDK[hm, :, g0:g0 + 512])
                        dfd.push(st2)
            dfd.flush()
            for g2 in range(8):
                slot, wv = load_w(C, w_in, 0, 32, C_DV + g2 * 256, 256)
                for tl in range(8):
                    gt = p * 8 + tl
                    pb = C["psr"].next()
                    for kc in range(32):
                        P.comp("pe", lambda e: e.matmul(
                            pb[:, 0:256], lhsT=nTp[:, kc, tl * 128:(tl + 1) * 128], rhs=wv[:, kc, :],
                            start=(kc == 0), stop=(kc == 31)), reads=[slot, nTp], writes=[pb], signal=(kc == 31))
                    ds = dvst.next()
                    if tl % 2 == 0:
                        P.comp("act", lambda e: e.copy(out=ds[:], in_=pb[:, 0:256]), reads=[pb], writes=[ds])
                    else:
                        P.comp("dve", lambda e: e.tensor_copy(out=ds[:], in_=pb[:, 0:256]), reads=[pb], writes=[ds])
                    P.dma("sp", sDV[g2, :, gt, :], ds[:], reads=[ds], writes=[sDV], nowaw=[sDV])
            for hg in range(2):
                slot, wv = load_w(C, w_ukv, 0, 4, hg * 2048, 2048)
                for half in range(2):
                    tok = toks[half]
                    g0 = t0 + half * 512
                    dfk = Deferred(2)
                    for hh in range(8):
                        h = hg * 8 + hh
                        pb = gemm_fm(C, slot, wv, 128, hh * 256, lambda kc, tok: ckvn[:, kc, tok], [ckvn], 4, tok,
                                     psring=C["psA"])
                        sq = square(C, pb, 128)

                        def st2(pb=pb, sq=sq, h=h, half=half, g0=g0):
                            rs = ss_rstd(C, cm, [(sq, 128), (kpsq[half], 128)], QK, psring=C["psB"])
                            finish_plain(C, pb, 128, 12, pcs, rs, sKn, sKn[h, :, g0:g0 + 512])
                            finish_mul(C, kprot[half], 64, rs, sKr, sKr[h, :, g0:g0 + 512])
                        dfk.push(st2)
                    dfk.flush()
                    for tt in range(4):
                        tl = half * 4 + tt
                        gt = (g0 // 128) + tt
                        vs = vst.next()
                        for pr in range(4):
                            pb = C["psr"].next()
                            for kc in range(4):
                                rhs = wv[:, kc, :].rearrange("p (h c) -> p h c", c=256)[:, 2 * pr:2 * pr + 2, 128:256]
                                P.comp("pe", lambda e: e.matmul(
                                    pb[:, 0:256].rearrange("p (h c) -> p h c", c=128),
                                    lhsT=ckvn[:, kc, tl * 128:(tl + 1) * 128], rhs=rhs,
                                    start=(kc == 0), stop=(kc == 3)), reads=[slot, ckvn], writes=[pb], signal=(kc == 3))
                            dstv = vs[:, pr * 256:(pr + 1) * 256]
                            if pr % 2 == 0:
                                P.comp("act", lambda e: e.copy(out=dstv, in_=pb[:, 0:256]),
                                       reads=[pb], writes=[vs], nowaw=[vs])
                            else:
                                P.comp("dve", lambda e: e.tensor_copy(out=dstv, in_=pb[:, 0:256]),
                                       reads=[pb], writes=[vs], nowaw=[vs])
                        P.dma("sp", sVm[hg * 8:(hg + 1) * 8, :, gt, :].rearrange("h p d -> p h d"),
                              vs[:, 0:1024].rearrange("p (h d) -> p h d", h=8), reads=[vs], writes=[sVm], nowaw=[sVm])
        ph.finish()

    def phase1b():
        ph = Phase()
        cm, pcs = load_consts(ph)
        C = make_p1_common(ph, cm, pcs)
        nTp = ph.sb("nTp", [128, 32, 1024], BF16)
        tab = ph.sb("tab", [128, 4, 1024], F32)
        cqraw = ph.sb("cqraw", [128, 6, 512], F32)
        cqsq = [ph.sb("cqsq", [128, 512], BF16) for _ in range(6)]
        cqn = ph.sb("cqn", [128, 6, 1024], BF16)
        P.dma("pool", nTp[:], nT[:, :, S:S + 1024].rearrange("k p t -> p k t"), reads=[nT], writes=[nTp])
        P.dma("pool", tab[:], tq[:], reads=[tq], writes=[tab])
        act_n = lambda kc, tok: nTp[:, kc, tok]
        for half in range(2):
            tok = slice(half * 512, (half + 1) * 512)
            g0 = half * 512
            for g2 in range(3):
                slot, wv = load_w(C, w_in, 0, 32, C_CQ + g2 * 256, 256)
                for j in range(2):
                    ch = g2 * 2 + j
                    pb = gemm_fm(C, slot, wv, 128, j * 128, act_n, [nTp], 32, tok)
                    P.comp("act", lambda e, pb=pb, ch=ch: e.activation(out=cqsq[ch][:], in_=pb[:], func=AF.Square),
                           reads=[pb], writes=[cqsq[ch]])
                    P.comp("act", lambda e, pb=pb, ch=ch: e.copy(out=cqraw[:, ch, :], in_=pb[:, :]),
                           reads=[pb], writes=[cqraw], nowaw=[cqraw])
            rs = ss_rstd(C, cm, [(cqsq[ch], 128) for ch in range(6)], QL)
            for ch in range(6):
                P.comp("dve", lambda e, ch=ch, rs=rs: e.scalar_tensor_tensor(
                    out=cqn[:, ch, tok], in0=cqraw[:, ch, :], scalar=pcs[:, ch:ch + 1], in1=rs[:],
                    op0=ALU.mult, op1=ALU.mult), reads=[cqraw, pcs, rs], writes=[cqn], nowaw=[cqn])
            act_q = lambda kc, tok: cqn[:, kc, tok]
            for hg in range(4):
                slot, wv = load_w(C, w_uq, 0, 6, hg * 768, 768)
                for hh in range(4):
                    h = hg * 4 + hh
                    pbn = gemm_fm(C, slot, wv, 128, hh * 192, act_q, [cqn], 6, tok)
                    pbr = gemm_fm(C, slot, wv, 64, hh * 192 + 128, act_q, [cqn], 6, tok)
                    sqn = square(C, pbn, 128)
                    sqr = square(C, pbr, 64)
                    xg = scaled_copy(C, pbr, 64, 11, pcs)
                    rs = ss_rstd(C, cm, [(sqn, 128), (sqr, 64)], QK)
                    prot = rot_mm(C, cm, xg, 64)
                    finish_plain(C, pbn, 128, 10, pcs, rs, sQn, sQn[h, :, g0:g0 + 512])
                    t3 = rope_combine(C, xg, prot, 64, tab[0:64, 2, tok], tab[0:64, 3, tok], tab)
                    finish_mul(C, t3, 64, rs, sQr, sQr[h, :, g0:g0 + 512])
            dfq = Deferred(1)
            for g2 in range(8):
                slot, wv = load_w(C, w_in, 0, 32, C_DQ + g2 * 256, 256)
                for j in range(2):
                    hm = g2 * 2 + j
                    pb = gemm_fm(C, slot, wv, 128, j * 128, act_n, [nTp], 32, tok)
                    sq = square(C, pb, 128)
                    xg = scaled_copy(C, pb, 128, 14, pcs)

                    def st2(sq=sq, xg=xg, hm=hm, g0=g0, tok=tok):
                        rs = ss_rstd(C, cm, [(sq, 128)], DHD)
                        prot = rot_mm(C, cm, xg, 128)
                        t3 = rope_combine(C, xg, prot, 128, tab[:, 0, tok], tab[:, 1, tok], tab)
                        finish_mul(C, t3, 128, rs, sDQ, sDQ[hm, :, g0:g0 + 512])
                    dfq.push(st2)
            dfq.flush()
        ph.finish()

    def phase2_3a():
        ph = Phase()
        cm, pcs = load_consts(ph)
        oT = ph.sb("oT", [128, 32, 1024], BF16)
        mk = ph.sb("mk", [128, 8, 128], BF16)
        P.dma("pool", mk[:], msk[:], reads=[msk], writes=[mk])
        lv = ph.sb("lv", [128, 512], F32)
        P.dma("pool", lv[:], lamv[:], reads=[lamv], writes=[lv])
        pr1 = ph.sb("pr1", [128, 256], F32)
        sm = ph.sb("sm", [128, 2], F32)
        ex = ph.sb("ex", [128, 2], F32)
        nlam = ph.sb("nlam", [128, 1], F32)
        gsub = ph.sb("gsub", [128, 2], F32)
        P.comp("dve", lambda e: e.tensor_tensor(out=pr1[:, 0:128], in0=lv[:, 0:128], in1=lv[:, 128:256], op=ALU.mult),
               reads=[lv], writes=[pr1])
        P.comp("dve", lambda e: e.tensor_tensor(out=pr1[:, 128:256], in0=lv[:, 256:384], in1=lv[:, 384:512], op=ALU.mult),
               reads=[lv, pr1], writes=[pr1])
        P.comp("dve", lambda e: e.reduce_sum(out=sm[:], in_=pr1[:].rearrange("p (a b) -> p a b", a=2), axis=AX.X),
               reads=[pr1], writes=[sm])
        P.comp("act", lambda e: e.activation(out=ex[:], in_=sm[:], func=AF.Exp), reads=[sm], writes=[ex])
        P.comp("dve", lambda e: e.scalar_tensor_tensor(out=nlam[:], in0=ex[:, 1:2], scalar=-LAMBDA_INIT, in1=ex[:, 0:1],
                                                       op0=ALU.add, op1=ALU.subtract), reads=[ex], writes=[nlam])
        P.comp("dve", lambda e: e.tensor_scalar(out=gsub[:], in0=pcs[:, 16:18], scalar1=(1.0 - LAMBDA_INIT), scalar2=None,
                                                op0=ALU.mult), reads=[pcs], writes=[gsub])
        NQ = 4
        pho = ph
        ph = Phase()
        kq = [ph.sb("kq", [128, S // NQ], BF16) for _ in range(NQ)]
        krq = [ph.sb("krq", [128, S // NQ], BF16) for _ in range(NQ)]
        vq = [ph.sb("vq", [128, 64 // NQ, 256], BF16) for _ in range(NQ)]
        qn = Ring([ph.sb("qn", [128, 1024], BF16) for _ in range(2)])
        qr = Ring([ph.sb("qr", [128, 1024], BF16) for _ in range(2)])
        for t_ in krq + qr.items:
            P.comp("dve", lambda e: e.memset(t_[64:128, :], 0.0), writes=[t_])
        ptr = Ring([ph.sb("pt", [128, 512], BF16) for _ in range(6)])
        sps = Ring([ph.ps("sps", [128, 512], F32) for _ in range(3)])
        lps1 = ph.ps("lps1", [128, 512], F32)
        acc = [ph.ps("acc", [128, 512], F32) for _ in range(4)]
        accL = [ph.sb("accL", [128, 512], F32) for _ in range(2)]
        lhi = Ring([ph.sb("lhi", [128, 512], BF16) for _ in range(2)])
        llo = Ring([ph.sb("llo", [128, 512], BF16) for _ in range(2)])
        rl = Ring([ph.sb("rl", [128, 512], F32) for _ in range(2)])
        t1s = ph.sb("t1s", [128, 2, 1024], F32)
        dsb = ph.sb("dsb", [128, 2, 512], F32)
        ub = Ring([ph.sb("ub", [128, 512], F32) for _ in range(2)])
        sq2 = [ph.sb("sq2", [128, 512], BF16) for _ in range(2)]
        rmb = ph.sb("rmb2", [128, 512], F32)
        rsb = ph.sb("rsb2", [128, 512], F32)
        KT = S // 128
        TPQ = KT // NQ

        def unit(kind, h, mp):
            nvc = 1 if kind == "mla" else 2
            vw = 128 * nvc
            scale = MLA_SCALE if kind == "mla" else DIFF_SCALE
            qnt = qn.next()
            if kind == "mla":
                qrt = qr.next()
                P.dma("pool", qnt[:], sQn[h], reads=[sQn], writes=[qnt])
                P.dma("pool", qrt[0:64, :], sQr[h], reads=[sQr], writes=[qrt], nowaw=[qrt])
            else:
                P.dma("pool", qnt[:], sDQ[2 * h + mp], reads=[sDQ], writes=[qnt])
            for q4 in range(NQ):
                ks = slice(q4 * (S // NQ), (q4 + 1) * (S // NQ))
                if kind == "mla":
                    P.dma("pool", kq[q4][:], sKn[h, :, ks], reads=[sKn], writes=[kq[q4]])
                    P.dma("pool", krq[q4][0:64, :], sKr[h, :, ks], reads=[sKr], writes=[krq[q4]], nowaw=[krq[q4]])
                    P.dma("pool", vq[q4][:, :, 0:128], sVm[h, :, q4 * TPQ:(q4 + 1) * TPQ, :], reads=[sVm], writes=[vq[q4]])
                else:
                    P.dma("pool", kq[q4][:], sDK[2 * h + mp, :, ks], reads=[sDK], writes=[kq[q4]])
                    P.dma("pool", vq[q4][:], sDV[h, :, q4 * TPQ:(q4 + 1) * TPQ, :], reads=[sDV], writes=[vq[q4]])
            plist = []
            for kt in range(KT):
                g = kt // 8
                for (c0, c1) in ([(128 * g, 512), (512, 1024)] if g < 4 else [(128 * g, 1024)]):
                    plist.append((kt, c0, c1))
            rope = kind == "mla"
            pts = {}

            def emit_s(i):
                kt, c0, c1 = plist[i]
                g, r = kt // 8, kt % 8
                q4, kl = kt // TPQ, kt % TPQ
                n = c1 - c0
                sp_ = sps.next()
                P.comp("pe", lambda e: e.matmul(
                    sp_[:, 0:n], lhsT=kq[q4][:, kl * 128:(kl + 1) * 128], rhs=qnt[:, c0:c1],
                    start=True, stop=(not rope)), reads=[kq[q4], qnt], writes=[sp_], signal=(not rope))
                if rope:
                    P.comp("pe", lambda e: e.matmul(
                        sp_[:, 0:n], lhsT=krq[q4][:, kl * 128:(kl + 1) * 128], rhs=qrt[:, c0:c1],
                        start=False, stop=True), reads=[krq[q4], qrt], writes=[sp_], signal=True)
                pt = ptr.next()
                P.comp("act", lambda e: e.activation(
                    out=pt[:, 0:n], in_=sp_[:, 0:n], func=AF.Exp, scale=scale), reads=[sp_], writes=[pt])
                if c0 == 128 * g:
                    P.comp("dve", lambda e: e.tensor_tensor(
                        out=pt[:, 0:128], in0=pt[:, 0:128], in1=mk[:, r, :], op=ALU.mult),
                        reads=[pt, mk], writes=[pt])
                b = 0 if c0 < 512 else 1
                off = c0 - 512 * b
                al = accL[b]
                if b == 1 and kt % 2 == 0:
                    pass
                elif kt == b:
                    P.comp("dve", lambda e: e.tensor_copy(out=al[:, off:off + n], in_=pt[:, 0:n]),
                           reads=[pt], writes=[al])
                else:
                    P.comp("dve", lambda e: e.tensor_tensor(
                        out=al[:, off:off + n], in0=al[:, off:off + n], in1=pt[:, 0:n], op=ALU.add),
                        reads=[pt, al], writes=[al])
                pts[i] = pt

            def emit_pv(i):
                kt, c0, c1 = plist[i]
                q4, kl = kt // TPQ, kt % TPQ
                n = c1 - c0
                b = 0 if c0 < 512 else 1
                off = c0 - 512 * b
                pt = pts.pop(i)
                first = kt == 0
                last = (kt == 31) if b == 0 else (kt == KT - 1)
                for vc in range(nvc):
                    a = acc[vc * 2 + b]
                    P.comp("pe", lambda e: e.matmul(
                        a[:, off:off + n], lhsT=vq[q4][:, kl, vc * 128:(vc + 1) * 128], rhs=pt[:, 0:n],
                        start=first, stop=last), reads=[vq[q4], pt], writes=[a], signal=(vc == nvc - 1))
                if b == 1 and kt % 2 == 0:
                    P.comp("pe", lambda e: e.matmul(
                        lps1[:, off:off + n], lhsT=cm[:, 1, :], rhs=pt[:, 0:n], start=first, stop=False),
                        reads=[cm, pt], writes=[lps1], signal=True)

            def epilogue(b):
                bs = slice(b * 512, (b + 1) * 512)
                rlt = rl.next()
                hi, lo = lhi.next(), llo.next()
                lps = lps1 if b == 1 else sps.next()
                P.comp("dve", lambda e: e.tensor_copy(out=hi[:], in_=accL[b][:]), reads=[accL[b]], writes=[hi])
                P.comp("dve", lambda e: e.tensor_tensor(out=lo[:], in0=accL[b][:], in1=hi[:], op=ALU.subtract),
                       reads=[accL[b], hi], writes=[lo])
                P.comp("pe", lambda e: e.matmul(lps[:, :], lhsT=cm[:, 1, :], rhs=hi[:], start=(b == 0), stop=False),
                       reads=[cm, hi], writes=[lps], signal=False)
                P.comp("pe", lambda e: e.matmul(lps[:, :], lhsT=cm[:, 1, :], rhs=lo[:], start=False, stop=True),
                       reads=[cm, lo], writes=[lps], signal=True)
                P.comp("dve", lambda e: e.reciprocal(out=rlt[:], in_=lps[:]), reads=[lps], writes=[rlt])
                if kind == "mla":
                    P.comp("dve", lambda e, rlt=rlt, b=b, bs=bs: e.tensor_tensor(
                        out=oT[:, h, bs], in0=acc[b][:], in1=rlt[:], op=ALU.mult),
                        reads=[acc[b], rlt], writes=[oT], nowaw=[oT])
                elif mp == 0:
                    for vc in range(2):
                        P.comp("dve", lambda e, rlt=rlt, b=b, bs=bs, vc=vc: e.tensor_tensor(
                            out=t1s[:, vc, bs], in0=acc[vc * 2 + b][:], in1=rlt[:], op=ALU.mult),
                            reads=[acc[vc * 2 + b], rlt], writes=[t1s], nowaw=[t1s])
                else:
                    for vc in range(2):
                        u = ub.next()
                        P.comp("dve", lambda e, rlt=rlt, b=b, vc=vc, u=u: e.tensor_tensor(
                            out=u[:], in0=acc[vc * 2 + b][:], in1=rlt[:], op=ALU.mult),
                            reads=[acc[vc * 2 + b], rlt], writes=[u])
                        P.comp("dve", lambda e, u=u, vc=vc, bs=bs: e.scalar_tensor_tensor(
                            out=dsb[:, vc, :], in0=u[:], scalar=nlam[:, 0:1], in1=t1s[:, vc, bs],
                            op0=ALU.mult, op1=ALU.add), reads=[u, nlam, t1s], writes=[dsb], nowaw=[dsb])
                        P.comp("act", lambda e, vc=vc: e.activation(out=sq2[vc][:], in_=dsb[:, vc, :], func=AF.Square),
                               reads=[dsb], writes=[sq2[vc]])
                    sp_ = sps.next()
                    for vc in range(2):
                        P.comp("pe", lambda e, sp_=sp_, vc=vc: e.matmul(
                            sp_[:, :], lhsT=cm[:, 1, :], rhs=sq2[vc][:], start=(vc == 0), stop=(vc == 1)),
                            reads=[cm, sq2[vc]], writes=[sp_], signal=(vc == 1))
                    P.comp("act", lambda e, sp_=sp_: e.activation(out=rmb[:], in_=sp_[:], func=AF.Sqrt,
                                                                  scale=1.0 / DV, bias=EPS), reads=[sp_], writes=[rmb])
                    P.comp("dve", lambda e: e.reciprocal(out=rsb[:], in_=rmb[:]), reads=[rmb], writes=[rsb])
                    for vc in range(2):
                        P.comp("dve", lambda e, vc=vc, bs=bs: e.scalar_tensor_tensor(
                            out=oT[:, 16 + 2 * h + vc, bs], in0=dsb[:, vc, :], scalar=gsub[:, vc:vc + 1], in1=rsb[:],
                            op0=ALU.mult, op1=ALU.mult), reads=[dsb, gsub, rsb], writes=[oT], nowaw=[oT])

            LOOK = 2
            for i in range(min(LOOK, len(plist))):
                emit_s(i)
            for i in range(len(plist)):
                if i + LOOK < len(plist):
                    emit_s(i + LOOK)
                emit_pv(i)
                if plist[i][0] == 31 and plist[i][1] < 512:
                    epilogue(0)
            epilogue(1)

        for h in range(NH):
            unit("mla", h, 0)
        for h in range(DH):
            unit("diff", h, 0)
            unit("diff", h, 1)

        ph.finish(extra=pho.ts)
        ph = Phase()
        ring = Ring([ph.sb("wr3", [128, 32, 512], BF16) for _ in range(2)])
        psw = Ring([ph.ps("psw", [128, 512], F32) for _ in range(8)])
        xs = Ring([ph.sb("xs", [128, 512], F32) for _ in range(4)])
        hs = Ring([ph.sb("hs", [128, 512], F32) for _ in range(4)])
        for gcol in range(D // 512):
            slot = ring.next()
            cs = slice(gcol * 512, (gcol + 1) * 512)
            for hk in range(2):
                P.dma("pool", slot[:, hk * 16:(hk + 1) * 16, :],
                      w_o[hk * 2048:(hk + 1) * 2048, cs].rearrange("(k p) n -> p k n", p=128),
                      reads=[w_o], writes=[slot], nowaw=[slot])
            for tl in range(8):
                xst = xs.next()
                P.dma("sp", xst[:], xo[tl * 128:(tl + 1) * 128, cs], reads=[xo], writes=[xst])
                pb = psw.next()
                for kc in range(32):
                    P.comp("pe", lambda e: e.matmul(
                        pb[:, :], lhsT=oT[:, kc, tl * 128:(tl + 1) * 128], rhs=slot[:, kc, :],
                        start=(kc == 0), stop=(kc == 31)), reads=[oT, slot], writes=[pb], signal=(kc == 31))
                hst = hs.next()
                P.comp("dve", lambda e: e.tensor_tensor(
                    out=hst[:], in0=pb[:, :], in1=xst[:], op=ALU.add), reads=[pb, xst], writes=[hst])
                P.dma("sp", hbuf[tl * 128:(tl + 1) * 128, cs], hst[:],
                      reads=[hst], writes=[hbuf], nowaw=[hbuf])
        ph.finish(extra=pho.ts)
        pho.es.close()

    def phase3c():
        ph = Phase()
        mT = ph.sb("mT", [128, 32, 1024], BF16)
        P.dma("pool", mT[:], nT[:, :, S:S + 1024].rearrange("k p t -> p k t"), reads=[nT], writes=[mT])
        actT = ph.sb("actT", [128, 16, 1024], BF16)
        ring = Ring([ph.sb("wr4", [128, 8192], BF16) for _ in range(4)])
        psr = Ring([ph.ps("psf", [128, 512], F32) for _ in range(8)])
        sg = Ring([ph.sb("sg", [128, 512], F32) for _ in range(3)])
        bs_ = Ring([ph.sb("bs", [128, 512], F32) for _ in range(6)])
        rs_ = Ring([ph.sb("rs", [128, 512], F32) for _ in range(4)])
        NCH = DFF // 128
        rounds = [(0, 16), (16, 16), (32, 16), (48, 16), (64, 16), (80, 6)]
        for ri, (f0, nf) in enumerate(rounds):
            for g2 in range(nf // 2):
                c0 = (f0 + g2 * 2) * 128
                sg_, wg = ring.next(), None
                wg = sg_[:, 0:8192].rearrange("p (k n) -> p k n", k=32)
                P.dma("pool", wg, w_gate[:, c0:c0 + 256].rearrange("(k p) n -> p k n", p=128), reads=[w_gate], writes=[sg_])
                su_ = ring.next()
                wu = su_[:, 0:8192].rearrange("p (k n) -> p k n", k=32)
                P.dma("pool", wu, w_up[:, c0:c0 + 256].rearrange("(k p) n -> p k n", p=128), reads=[w_up], writes=[su_])
                for j in range(2):
                    fl = g2 * 2 + j
                    for half in range(2):
                        tok = slice(half * 512, (half + 1) * 512)
                        pg = psr.next()
                        for kc in range(32):
                            P.comp("pe", lambda e, pg=pg, kc=kc, j=j, tok=tok, wg=wg: e.matmul(
                                pg[:, :], lhsT=wg[:, kc, j * 128:(j + 1) * 128], rhs=mT[:, kc, tok],
                                start=(kc == 0), stop=(kc == 31)), reads=[sg_, mT], writes=[pg], signal=(kc == 31))
                        pu = psr.next()
                        for kc in range(32):
                            P.comp("pe", lambda e, pu=pu, kc=kc, j=j, tok=tok, wu=wu: e.matmul(
                                pu[:, :], lhsT=wu[:, kc, j * 128:(j + 1) * 128], rhs=mT[:, kc, tok],
                                start=(kc == 0), stop=(kc == 31)), reads=[su_, mT], writes=[pu], signal=(kc == 31))
                        sgt = sg.next()
                        P.comp("act", lambda e, pg=pg, sgt=sgt: e.activation(out=sgt[:], in_=pg[:], func=AF.Silu),
                               reads=[pg], writes=[sgt])
                        P.comp("dve", lambda e, pu=pu, sgt=sgt, fl=fl, tok=tok: e.tensor_tensor(
                            out=actT[:, fl, tok], in0=pu[:], in1=sgt[:], op=ALU.mult),
                            reads=[pu, sgt], writes=[actT], nowaw=[actT])
            srcb = hbuf if ri == 0 else out
            items = [(gcol, tl) for gcol in range(D // 512) for tl in range(8)]
            bases = {}
            nb = [0]

            def emit_base(upto):
                while nb[0] < min(upto, len(items)):
                    gcol, tl = items[nb[0]]
                    base = bs_.next()
                    P.dma("sp", base[:], srcb[tl * 128:(tl + 1) * 128, gcol * 512:(gcol + 1) * 512],
                          reads=[srcb], writes=[base])
                    bases[nb[0]] = base
                    nb[0] += 1

            for ii, (gcol, tl) in enumerate(items):
                if tl == 0:
                    slot = ring.next()
                    wd = slot[:, 0:nf * 512].rearrange("p (k n) -> p k n", k=nf)
                    P.dma("pool", wd, w_down[f0 * 128:(f0 + nf) * 128, gcol * 512:(gcol + 1) * 512].rearrange(
                        "(k p) n -> p k n", p=128), reads=[w_down], writes=[slot])
                emit_base(ii + 3)
                base = bases.pop(ii)
                pb = psr.next()
                for f in range(nf):
                    P.comp("pe", lambda e: e.matmul(
                        pb[:, :], lhsT=actT[:, f, tl * 128:(tl + 1) * 128], rhs=wd[:, f, :],
                        start=(f == 0), stop=(f == nf - 1)), reads=[actT, slot], writes=[pb], signal=(f == nf - 1))
                res = rs_.next()
                P.comp("dve", lambda e: e.tensor_tensor(
                    out=res[:], in0=pb[:, :], in1=base[:], op=ALU.add), reads=[pb, base], writes=[res])
                P.dma("sp", out[tl * 128:(tl + 1) * 128, gcol * 512:(gcol + 1) * 512], res[:],
                      reads=[res], writes=[out], nowaw=[out])
        ph.finish()

    norm_rows(x, S, 0, 0)
    norm_rows(xo, TOWN, 0, S)
    phase1a()
    phase1b()
    phase2_3a()
    norm_rows(hbuf, TOWN, 1, S)
    phase3c()
    return nc


DEBUG_OUT = set()

_CACHE = {}


def _rope_tab(pos, dim):
    half = dim // 2
    inv = 10000.0 ** (-np.arange(half, dtype=np.float64) / half)
    ang = pos.astype(np.float64)[None, :] * np.concatenate([inv, inv])[:, None]
    return np.cos(ang).astype(np.float32), np.sin(ang).astype(np.float32)


def _rot_lhsT(dim):
    h = dim // 2
    m = np.zeros((128, 128), np.float32)
    for mm in range(dim):
        if mm < h:
            m[mm + h, mm] = -1.0
        else:
            m[mm - h, mm] = 1.0
    return m


def kernel(**inputs):
    f = lambda k: np.ascontiguousarray(np.asarray(inputs[k], dtype=np.float32))
    x = f("x")[0]
    bf = ml_dtypes.bfloat16
    pc = np.zeros((128, 32), np.float32)
    pc[:, 0:6] = f("q_latent_norm_g")[0].reshape(6, 128).T
    pc[:, 6:10] = f("kv_latent_norm_g")[0].reshape(4, 128).T
    gq, gk = f("mla_q_norm_g")[0], f("mla_k_norm_g")[0]
    pc[:, 10] = gq[:128]
    pc[:64, 11] = gq[128:]
    pc[:, 12] = gk[:128]
    pc[:64, 13] = gk[128:]
    pc[:, 14] = f("diff_q_norm_g")[0]
    pc[:, 15] = f("diff_k_norm_g")[0]
    pc[:, 16:18] = f("diff_subln_g")[0].reshape(2, 128).T
    gb = np.stack([np.broadcast_to(f("attn_norm_g")[0], (128, D)), np.broadcast_to(f("ffn_norm_g")[0], (128, D))])
    gb = np.ascontiguousarray(gb)
    lamv = np.concatenate([f("lambda_q1")[0], f("lambda_k1")[0], f("lambda_q2")[0], f("lambda_k2")[0]])
    lamv = np.ascontiguousarray(np.broadcast_to(lamv, (128, 512)))
    cmat = np.zeros((4, 128, 128), np.float32)
    cmat[0] = np.eye(128)
    cmat[1] = 1.0
    cmat[2] = _rot_lhsT(128)
    cmat[3] = _rot_lhsT(64)
    cmat = cmat.astype(bf)
    pos = np.arange(S)
    c128, s128 = _rope_tab(pos, 128)
    c64, s64 = _rope_tab(pos, 64)
    tk = np.zeros((128, 4, S), np.float32)
    tk[:, 0], tk[:, 1] = c128, s128
    tk[:64, 2], tk[:64, 3] = c64, s64
    shared = {
        "x": x, "w_in": f("w_in")[0], "w_uq": f("w_uq")[0], "w_ukv": f("w_ukv")[0], "w_o": f("w_o")[0],
        "w_gate": f("w_gate")[0], "w_up": f("w_up")[0], "w_down": f("w_down")[0],
        "gb": gb, "pc": pc, "lamv": lamv, "cmat": cmat, "tk": tk,
    }
    in_maps = []
    own_idx = []
    kk = np.arange(128)
    for c in range(NCORES):
        idx = np.concatenate([np.arange((8 * j + c) * 128, (8 * j + c + 1) * 128) for j in range(8)])
        own_idx.append(idx)
        msk = np.zeros((128, 8, 128), np.float32)
        for r in range(8):
            if r < c:
                msk[:, r, :] = 1.0
            elif r == c:
                msk[:, r, :] = ((kk[:, None] // 64) <= (kk[None, :] // 64)).astype(np.float32)
        m = dict(shared)
        m["xo"] = np.ascontiguousarray(x[idx])
        m["tq"] = np.ascontiguousarray(tk[:, :, idx])
        m["msk"] = msk.astype(bf)
        in_maps.append(m)
    if "nc" not in _CACHE:
        _CACHE["nc"] = build_program()
    res = run_bass_kernel_spmd(_CACHE["nc"], in_maps, core_ids=list(range(NCORES)))
    _CACHE["res"] = res
    outp = np.empty((S, D), np.float32)
    for c in range(NCORES):
        outp[own_idx[c]] = np.asarray(res.results[c]["out"], dtype=np.float32)
    return outp[None]
```

```python
import math
from contextlib import ExitStack
import numpy as np
import ml_dtypes
import concourse.bass as bass
import concourse.mybir as mybir
from concourse.bass_utils import run_bass_kernel_spmd

F32 = mybir.dt.float32
BF16 = mybir.dt.bfloat16
ALU = mybir.AluOpType
AF = mybir.ActivationFunctionType
AX = mybir.AxisListType

NCORES = 8
S = 8192
D = 4096
TOWN = 1024
EPS = 1e-6
QL, KVL, ROPE = 768, 512, 64
NH, QK, NOPE, VD = 16, 192, 128, 128
DH, DHD, DV = 8, 128, 256
D_IN = 7488
DFF = 11008
C_CQ, C_CKV, C_KPE, C_DQ, C_DK, C_DV = 0, 768, 1280, 1344, 3392, 5440
MLA_SCALE = 1.0 / math.sqrt(QK)
DIFF_SCALE = 1.0 / math.sqrt(DHD)
LAMBDA_INIT = 0.8 - 0.6 * math.exp(0.0)
DEBUG = False
_UID = [0]


class Buf:
    def __init__(self, name, persistent=False):
        self.name = name
        self.lw = {}
        self.lr = {}
        self.dsem = None
        self.persistent = persistent


class T:
    def __init__(self, h, name, persistent=False):
        self.h = h
        self.buf = Buf(name, persistent)

    def __getitem__(self, idx):
        return self.h[idx]


class Op:
    __slots__ = ("eng", "fn", "deps", "signal", "cidx", "dsem", "dval", "kind")


class _Rec:
    def __init__(self):
        self.call = None

    def __getattr__(self, name):
        def f(*a, **k):
            self.call = (name, a, k)
            return self
        return f


def _freeze(fn):
    r = _Rec()
    fn(r)
    name, a, k = r.call
    return lambda e: getattr(e, name)(*a, **k)


class Prog:
    COMPUTE = ("pe", "act", "dve")
    ALL = ("pe", "act", "dve", "pool", "sp")

    def __init__(self, nc):
        self.nc = nc
        self.csem = {e: nc.alloc_semaphore(name=f"c_{e}") for e in self.COMPUTE}
        self.cbase = {e: 0 for e in self.COMPUTE}
        self.dsems = [nc.alloc_semaphore(name=f"d_{i}") for i in range(94)]
        self.dcount = [0] * len(self.dsems)
        self.dfree = list(range(len(self.dsems)))
        self.reset_phase()
        self.persist = []

    def reset_phase(self):
        self.streams = {e: [] for e in self.ALL}
        self.ncomp = {e: 0 for e in self.COMPUTE}
        self.phase_bufs = []

    def _deps(self, op, reads, writes, nowaw):
        deps = {}

        def add(d):
            for k, v in d.items():
                if deps.get(k, -1) < v:
                    deps[k] = v

        for t in reads:
            add(t.buf.lw)
        for t in writes:
            add(t.buf.lr)
            if t not in nowaw:
                add(t.buf.lw)
        if op.eng == "pe":
            deps.pop("pe", None)
        op.deps = deps

    def _commit(self, clock, val, reads, writes, nowaw):
        for t in reads:
            b = t.buf
            if b.lr.get(clock, -1) < val:
                b.lr[clock] = val
        for t in writes:
            b = t.buf
            if t in nowaw:
                if b.lw.get(clock, -1) < val:
                    b.lw[clock] = val
            else:
                b.lw = {clock: val}
                b.lr = {}

    def comp(self, eng, fn, reads=(), writes=(), signal=True, nowaw=()):
        op = Op()
        op.eng, op.fn, op.signal, op.kind = eng, _freeze(fn), signal, "c"
        self._deps(op, reads, writes, nowaw)
        op.cidx = self.ncomp[eng]
        self.ncomp[eng] += 1
        self._commit(eng, op.cidx, reads, writes, nowaw)
        self.streams[eng].append(op)
        return op

    def dma(self, queue, out_ap, in_ap, reads=(), writes=(), nowaw=()):
        assert len(writes) == 1
        dst = writes[0].buf
        if dst.dsem is None:
            dst.dsem = self.dfree.pop(0)
            if not dst.persistent:
                self.phase_bufs.append(dst)
        op = Op()
        op.eng, op.kind, op.signal = queue, "d", True
        op.fn = lambda e: e.dma_start(out=out_ap, in_=in_ap)
        self._deps(op, reads, writes, nowaw)
        self.dcount[dst.dsem] += 16
        op.dsem, op.dval = dst.dsem, self.dcount[dst.dsem]
        self._commit(("d", dst.dsem), op.dval, reads, writes, nowaw)
        self.streams[queue].append(op)
        return op

    def emit(self, all_ts):
        nc = self.nc
        sigval = {}
        for e in self.COMPUTE:
            ops = self.streams[e]
            if ops:
                ops[-1].signal = True
            vals = [0] * len(ops)
            cnt = self.cbase[e]
            for i, op in enumerate(ops):
                if op.signal:
                    cnt += 1
                vals[i] = cnt
            nxt = None
            res = [0] * len(ops)
            for i in range(len(ops) - 1, -1, -1):
                if ops[i].signal:
                    nxt = vals[i]
                res[i] = nxt
            sigval[e] = res
            self.cbase_next = getattr(self, "cbase_next", {})
            self.cbase_next[e] = cnt
        final = []
        for e in self.COMPUTE:
            if self.streams[e]:
                final.append((self.csem[e], self.cbase_next[e]))
        used_d = set()
        for q in ("pool", "sp"):
            for op in self.streams[q]:
                used_d.add(op.dsem)
        for dsi in used_d:
            final.append((self.dsems[dsi], self.dcount[dsi]))

        def run_stream(ename):
            ops = self.streams[ename]

            def body(eng):
                waited = {}

                def wait(sem, val):
                    key = id(sem)
                    if waited.get(key, -1) >= val:
                        return
                    waited[key] = val
                    eng.wait_ge(sem, val)

                for op in ops:
                    for clock, v in op.deps.items():
                        if isinstance(clock, tuple):
                            wait(self.dsems[clock[1]], v)
                        else:
                            wait(self.csem[clock], sigval[clock][v])
                    ins = op.fn(eng)
                    if op.kind == "d":
                        ins.then_inc(self.dsems[op.dsem], 16)
                    elif op.signal:
                        ins.then_inc(self.csem[op.eng], 1)
                for sem, val in final:
                    wait(sem, val)

            return body

        with nc.Block() as block:
            block.tensor(run_stream("pe"))
            block.scalar(run_stream("act"))
            block.vector(run_stream("dve"))
            block.gpsimd(run_stream("pool"))
            block.sync(run_stream("sp"))
        for e in self.COMPUTE:
            self.cbase[e] = self.cbase_next[e]
        for b in self.phase_bufs:
            self.dfree.append(b.dsem)
            b.dsem = None
        for t in all_ts:
            t.buf.lw = {}
            t.buf.lr = {}
        self.reset_phase()


def build_program():
    nc = bass.Bass("TRN2", target_bir_lowering=False)
    P = Prog(nc)
    persist_ts = []

    def dram(name, shape, dt, kind="Internal"):
        if DEBUG and kind == "Internal" and name in DEBUG_OUT:
            kind = "ExternalOutput"
        t = T(nc.dram_tensor(name, shape, dt, kind=kind).ap(), name, persistent=True)
        persist_ts.append(t)
        return t

    x = dram("x", [S, D], F32, "ExternalInput")
    xo = dram("xo", [TOWN, D], F32, "ExternalInput")
    w_in = dram("w_in", [D, D_IN], F32, "ExternalInput")
    w_uq = dram("w_uq", [QL, NH * QK], F32, "ExternalInput")
    w_ukv = dram("w_ukv", [KVL, NH * (NOPE + VD)], F32, "ExternalInput")
    w_o = dram("w_o", [D, D], F32, "ExternalInput")
    w_gate = dram("w_gate", [D, DFF], F32, "ExternalInput")
    w_up = dram("w_up", [D, DFF], F32, "ExternalInput")
    w_down = dram("w_down", [DFF, D], F32, "ExternalInput")
    gb = dram("gb", [2, 128, D], F32, "ExternalInput")
    pc = dram("pc", [128, 32], F32, "ExternalInput")
    lamv = dram("lamv", [128, 512], F32, "ExternalInput")
    cmat = dram("cmat", [4, 128, 128], BF16, "ExternalInput")
    tk = dram("tk", [128, 4, S], F32, "ExternalInput")
    tq = dram("tq", [128, 4, TOWN], F32, "ExternalInput")
    msk = dram("msk", [128, 8, 128], BF16, "ExternalInput")
    out = dram("out", [TOWN, D], F32, "ExternalOutput")
    nT = dram("nT", [32, 128, S + TOWN], BF16)
    sKn = dram("sKn", [NH, 128, S], BF16)
    sKr = dram("sKr", [NH, 64, S], BF16)
    sVm = dram("sVm", [NH, 128, 64, VD], BF16)
    sDK = dram("sDK", [2 * DH, 128, S], BF16)
    sDV = dram("sDV", [DH, 128, 64, DV], BF16)
    sQn = dram("sQn", [NH, 128, TOWN], BF16)
    sQr = dram("sQr", [NH, 64, TOWN], BF16)
    sDQ = dram("sDQ", [2 * DH, 128, TOWN], BF16)
    hbuf = dram("hbuf", [TOWN, D], F32)

    class Phase:
        def __init__(self):
            self.es = ExitStack()
            self.ts = []
            self.n = 0

        def sb(self, name, shape, dt):
            _UID[0] += 1
            h = self.es.enter_context(nc.sbuf_tensor(f"{name}_{_UID[0]}", shape, dt))
            t = T(h, name)
            self.ts.append(t)
            return t

        def ps(self, name, shape, dt):
            _UID[0] += 1
            h = self.es.enter_context(nc.psum_tensor(f"{name}_{_UID[0]}", shape, dt))
            t = T(h, name)
            self.ts.append(t)
            return t

        def finish(self, extra=()):
            P.emit(self.ts + list(extra) + persist_ts)
            self.es.close()

    class Deferred:
        def __init__(self, depth):
            self.depth = depth
            self.q = []

        def push(self, fn):
            self.q.append(fn)
            while len(self.q) > self.depth:
                self.q.pop(0)()

        def flush(self):
            while self.q:
                self.q.pop(0)()

    class Ring:
        def __init__(self, items):
            self.items = items
            self.i = 0

        def next(self):
            t = self.items[self.i % len(self.items)]
            self.i += 1
            return t

    def load_consts(ph):
        cm = ph.sb("cm", [128, 4, 128], BF16)
        P.dma("pool", cm[:], cmat[:].rearrange("a p q -> p a q"), reads=[cmat], writes=[cm])
        pcs = ph.sb("pcs", [128, 32], F32)
        P.dma("pool", pcs[:], pc[:], reads=[pc], writes=[pcs])
        return cm, pcs

    def norm_rows(src, ntok, gidx, col0):
        ph = Phase()
        cm, pcs = load_consts(ph)
        gbt = ph.sb("gbt", [128, D], F32)
        P.dma("pool", gbt[:], gb[gidx], reads=[gb], writes=[gbt])
        xt = Ring([ph.sb("xt", [128, D], F32) for _ in range(3)])
        nt = Ring([ph.sb("nt", [128, D], BF16) for _ in range(3)])
        junk = ph.sb("junk", [128, D], BF16)
        ssr = Ring([ph.sb("ss", [128, 1], F32) for _ in range(3)])
        rmr = Ring([ph.sb("rm", [128, 1], F32) for _ in range(3)])
        rsr = Ring([ph.sb("rs", [128, 1], F32) for _ in range(3)])
        stg = Ring([ph.sb("stg", [128, 32, 512], BF16) for _ in range(2)])
        pst = Ring([ph.ps("pst", [128, 1024], BF16) for _ in range(4)])
        ident = cm[:, 0, :]
        ntiles = ntok // 128
        ntts = {}

        def stage_a(ti):
            r0 = ti * 128
            xtt, ntt, ss, rm, rs = xt.next(), nt.next(), ssr.next(), rmr.next(), rsr.next()
            P.dma("pool", xtt[:], src[r0:r0 + 128, :], reads=[src], writes=[xtt])
            P.comp("dve", lambda e: e.memset(ss[:], 0.0), writes=[ss])
            P.comp("act", lambda e: e.activation(out=junk[:], in_=xtt[:], func=AF.Square, accum_out=ss[:]),
                   reads=[xtt, ss], writes=[junk, ss])
            P.comp("act", lambda e: e.activation(out=rm[:], in_=ss[:], func=AF.Sqrt, scale=1.0 / D, bias=EPS),
                   reads=[ss], writes=[rm])
            P.comp("dve", lambda e: e.reciprocal(out=rs[:], in_=rm[:]), reads=[rm], writes=[rs])
            P.comp("dve", lambda e: e.scalar_tensor_tensor(
                out=ntt[:], in0=xtt[:], scalar=rs[:, 0:1], in1=gbt[:], op0=ALU.mult, op1=ALU.mult),
                reads=[xtt, rs, gbt], writes=[ntt])
            ntts[ti] = ntt

        sts = {}

        def stage_b(ti):
            g, t = ti // 4, ti % 4
            if t == 0:
                sts[g] = stg.next()
            st = sts[g]
            ntt = ntts.pop(ti)
            for q in range(4):
                pb = pst.next()
                for j in range(8):
                    kc = q * 8 + j
                    P.comp("pe", lambda e: e.transpose(
                        out=pb[:, j * 128:(j + 1) * 128], in_=ntt[:, kc * 128:(kc + 1) * 128], identity=ident),
                        reads=[ntt, cm], writes=[pb], signal=(j == 7))
                dst = st[:, q * 8:(q + 1) * 8, t * 128:(t + 1) * 128]
                srcv = pb[:].rearrange("p (a b) -> p a b", a=8)
                if q % 2 == 0:
                    P.comp("act", lambda e: e.copy(out=dst, in_=srcv), reads=[pb], writes=[st], nowaw=[st])
                else:
                    P.comp("dve", lambda e: e.tensor_copy(out=dst, in_=srcv), reads=[pb], writes=[st], nowaw=[st])
            if t == 3:
                c0 = col0 + g * 512
                P.dma("sp", nT[:, :, c0:c0 + 512].rearrange("k p t -> p k t"), st[:], reads=[st], writes=[nT], nowaw=[nT])

        stage_a(0)
        for ti in range(ntiles):
            if ti + 1 < ntiles:
                stage_a(ti + 1)
            stage_b(ti)
        ph.finish()

    def make_p1_common(ph, cm, pcs, nring=3):
        C = {}
        C["ring"] = Ring([ph.sb("wr", [128, 8192], BF16) for _ in range(nring)])
        C["psr"] = Ring([ph.ps("psb", [128, 512], F32) for _ in range(8)])
        C["psA"] = Ring(C["psr"].items[0:5])
        C["psB"] = Ring(C["psr"].items[5:8])
        C["sq"] = Ring([ph.sb("sq", [128, 512], BF16) for _ in range(5)])
        C["xg"] = Ring([ph.sb("xg", [128, 512], BF16) for _ in range(4)])
        C["rm"] = Ring([ph.sb("rmb", [128, 512], F32) for _ in range(2)])
        C["rs"] = Ring([ph.sb("rsb", [128, 512], F32) for _ in range(3)])
        C["t1"] = Ring([ph.sb("t1", [128, 512], F32) for _ in range(2)])
        C["t2"] = Ring([ph.sb("t2", [128, 512], F32) for _ in range(2)])
        C["ob"] = Ring([ph.sb("ob", [128, 512], BF16) for _ in range(4)])
        return C

    def load_w(C, wsrc, r0, nkc, c0, ncols):
        slot = C["ring"].next()
        v = slot[:, 0:nkc * ncols].rearrange("p (k n) -> p k n", k=nkc)
        P.dma("pool", v, wsrc[r0:r0 + nkc * 128, c0:c0 + ncols].rearrange("(k p) n -> p k n", p=128),
              reads=[wsrc], writes=[slot])
        return slot, v

    def gemm_fm(C, slot, wv, mcols, m0, act, abufs, nkc, tok, psring=None):
        pb = (psring or C["psr"]).next()
        for kc in range(nkc):
            P.comp("pe", lambda e, pb=pb, kc=kc: e.matmul(
                pb[0:mcols, :], lhsT=wv[:, kc, m0:m0 + mcols], rhs=act(kc, tok),
                start=(kc == 0), stop=(kc == nkc - 1)),
                reads=[slot] + abufs, writes=[pb], signal=(kc == nkc - 1))
        return pb

    def square(C, pb, rows):
        sq = C["sq"].next()
        P.comp("act", lambda e: e.activation(out=sq[0:rows, :], in_=pb[0:rows, :], func=AF.Square),
               reads=[pb], writes=[sq])
        return sq

    def scaled_copy(C, pb, rows, gcol, pcs):
        xg = C["xg"].next()
        P.comp("act", lambda e: e.activation(out=xg[0:rows, :], in_=pb[0:rows, :], func=AF.Copy,
                                             scale=pcs[0:rows, gcol:gcol + 1]),
               reads=[pb, pcs], writes=[xg])
        return xg

    def ss_rstd(C, cm, terms, dnorm, psring=None):
        pb = (psring or C["psr"]).next()
        for i, (sq, rows) in enumerate(terms):
            P.comp("pe", lambda e, sq=sq, rows=rows, i=i: e.matmul(
                pb[:, :], lhsT=cm[0:rows, 1, :], rhs=sq[0:rows, :],
                start=(i == 0), stop=(i == len(terms) - 1)),
                reads=[sq, cm], writes=[pb], signal=(i == len(terms) - 1))
        rm = C["rm"].next()
        P.comp("act", lambda e: e.activation(out=rm[:], in_=pb[:], func=AF.Sqrt, scale=1.0 / dnorm, bias=EPS),
               reads=[pb], writes=[rm])
        rs = C["rs"].next()
        P.comp("dve", lambda e: e.reciprocal(out=rs[:], in_=rm[:]), reads=[rm], writes=[rs])
        return rs

    def rot_mm(C, cm, xg, rows, psring=None):
        pb = (psring or C["psr"]).next()
        ri = 2 if rows == 128 else 3
        P.comp("pe", lambda e: e.matmul(pb[0:rows, :], lhsT=cm[0:rows, ri, 0:rows], rhs=xg[0:rows, :],
                                        start=True, stop=True), reads=[xg, cm], writes=[pb])
        return pb

    def rope_combine(C, xg, prot, rows, cosap, sinap, tabbuf):
        t1 = C["t1"].next()
        t2 = C["t2"].next()
        P.comp("dve", lambda e: e.tensor_tensor(out=t1[0:rows, :], in0=xg[0:rows, :], in1=cosap, op=ALU.mult),
               reads=[xg, tabbuf], writes=[t1])
        P.comp("dve", lambda e: e.tensor_tensor(out=t2[0:rows, :], in0=prot[0:rows, :], in1=sinap, op=ALU.mult),
               reads=[prot, tabbuf], writes=[t2])
        P.comp("dve", lambda e: e.tensor_tensor(out=t1[0:rows, :], in0=t1[0:rows, :], in1=t2[0:rows, :], op=ALU.add),
               reads=[t1, t2], writes=[t1])
        return t1

    def finish_mul(C, src, rows, rs, dst_dram, dst_ap, extra_reads=()):
        ob = C["ob"].next()
        P.comp("dve", lambda e: e.tensor_tensor(out=ob[0:rows, :], in0=src[0:rows, :], in1=rs[0:rows, :], op=ALU.mult),
               reads=[src, rs] + list(extra_reads), writes=[ob])
        P.dma("sp", dst_ap, ob[0:rows, :], reads=[ob], writes=[dst_dram], nowaw=[dst_dram])

    def finish_plain(C, pb, rows, gcol, pcs, rs, dst_dram, dst_ap):
        ob = C["ob"].next()
        P.comp("dve", lambda e: e.scalar_tensor_tensor(
            out=ob[0:rows, :], in0=pb[0:rows, :], scalar=pcs[0:rows, gcol:gcol + 1], in1=rs[0:rows, :],
            op0=ALU.mult, op1=ALU.mult), reads=[pb, pcs, rs], writes=[ob])
        P.dma("sp", dst_ap, ob[0:rows, :], reads=[ob], writes=[dst_dram], nowaw=[dst_dram])

    def latent_norm(C, cm, pcs, wsrc, ccol0, nch, gcol0, dnorm, nTp, raw, outn, half):
        tok = slice(half * 512, (half + 1) * 512)
        sqs = []
        for g2 in range(nch // 2):
            slot, wv = load_w(C, wsrc, 0, 32, ccol0 + g2 * 256, 256)
            for j in range(2):
                ch = g2 * 2 + j
                pb = gemm_fm(C, slot, wv, 128, j * 128, lambda kc, tok: nTp[:, kc, tok], [nTp], 32, tok)
                sq = square(C, pb, 128)
                P.comp("act", lambda e, pb=pb, ch=ch: e.copy(out=raw[:, ch, :], in_=pb[:, :]),
                       reads=[pb], writes=[raw], nowaw=[raw])
                sqs.append(sq)
                if len(sqs) == 3 or ch == nch - 1:
                    pass
        return sqs

    def phase1a():
        ph = Phase()
        cm, pcs = load_consts(ph)
        C = make_p1_common(ph, cm, pcs, nring=3)
        nTp = ph.sb("nTp", [128, 32, 1024], BF16)
        tabs = [ph.sb("tab", [128, 4, 512], F32) for _ in range(2)]
        ckraw = [ph.sb("ckraw", [128, 4, 512], F32) for _ in range(2)]
        cksq = [[ph.sb("cksq", [128, 512], BF16) for _ in range(4)] for _ in range(2)]
        ckvn = ph.sb("ckvn", [128, 4, 1024], BF16)
        kpsq = [ph.sb("kpsq", [128, 512], BF16) for _ in range(2)]
        for t_ in kpsq:
            P.comp("dve", lambda e: e.memset(t_[64:128, :], 0.0), writes=[t_])
        kprot = [ph.sb("kprot", [64, 512], F32) for _ in range(2)]
        vst = Ring([ph.sb("vst", [128, 1024], BF16) for _ in range(2)])
        dvst = Ring([ph.sb("dvst", [128, 256], BF16) for _ in range(3)])
        act_n = lambda kc, tok: nTp[:, kc, tok]
        toks = [slice(0, 512), slice(512, 1024)]
        for p in range(S // 1024):
            t0 = p * 1024
            P.dma("pool", nTp[:], nT[:, :, t0:t0 + 1024].rearrange("k p t -> p k t"), reads=[nT], writes=[nTp])
            for half in range(2):
                g0 = t0 + half * 512
                P.dma("pool", tabs[half][:], tk[:, :, g0:g0 + 512], reads=[tk], writes=[tabs[half]])
            for g2 in range(2):
                slot, wv = load_w(C, w_in, 0, 32, C_CKV + g2 * 256, 256)
                for half in range(2):
                    for j in range(2):
                        ch = g2 * 2 + j
                        pb = gemm_fm(C, slot, wv, 128, j * 128, act_n, [nTp], 32, toks[half])
                        sqd = cksq[half][ch]
                        P.comp("act", lambda e: e.activation(out=sqd[:], in_=pb[:], func=AF.Square),
                               reads=[pb], writes=[sqd])
                        P.comp("act", lambda e: e.copy(out=ckraw[half][:, ch, :], in_=pb[:, :]),
                               reads=[pb], writes=[ckraw[half]], nowaw=[ckraw[half]])
            for half in range(2):
                rs = ss_rstd(C, cm, [(cksq[half][ch], 128) for ch in range(4)], KVL)
                for ch in range(4):
                    P.comp("dve", lambda e: e.scalar_tensor_tensor(
                        out=ckvn[:, ch, toks[half]], in0=ckraw[half][:, ch, :], scalar=pcs[:, 6 + ch:7 + ch], in1=rs[:],
                        op0=ALU.mult, op1=ALU.mult), reads=[ckraw[half], pcs, rs], writes=[ckvn], nowaw=[ckvn])
            slot, wv = load_w(C, w_in, 0, 32, C_KPE, 64)
            for half in range(2):
                tab = tabs[half]
                pb = gemm_fm(C, slot, wv, 64, 0, act_n, [nTp], 32, toks[half])
                P.comp("act", lambda e: e.activation(out=kpsq[half][0:64, :], in_=pb[0:64, :], func=AF.Square),
                       reads=[pb], writes=[kpsq[half]], nowaw=[kpsq[half]])
                xg = scaled_copy(C, pb, 64, 13, pcs)
                prot = rot_mm(C, cm, xg, 64)
                t3 = rope_combine(C, xg, prot, 64, tab[0:64, 2, :], tab[0:64, 3, :], tab)
                P.comp("dve", lambda e: e.tensor_copy(out=kprot[half][:], in_=t3[0:64, :]), reads=[t3], writes=[kprot[half]])
            banks = C["psr"].items
            psD = Ring(banks[0:5])
            psK = Ring(banks[5:7])
            psS = Ring(banks[7:8])
            ring2 = Ring(C["ring"].items[0:2])
            dfd = Deferred(1)
            dfk = Deferred(1)
            act_c = lambda kc, tok: ckvn[:, kc, tok]
            for hg in range(2):
                uslot = C["ring"].items[2]
                uwv = uslot[:, 0:4 * 2048].rearrange("p (k n) -> p k n", k=4)
                P.dma("pool", uwv, w_ukv[0:512, hg * 2048:(hg + 1) * 2048].rearrange("(k p) n -> p k n", p=128),
                      reads=[w_ukv], writes=[uslot])
                vgroups = [(tl, pr) for tl in range(8) for pr in range(4)]
                vstage = {}
                vi = 0
                for g4 in range(4):
                    g2 = hg * 4 + g4
                    slot = ring2.next()
                    wv = slot[:, 0:32 * 256].rearrange("p (k n) -> p k n", k=32)
                    P.dma("pool", wv, w_in[:, C_DK + g2 * 256:C_DK + (g2 + 1) * 256].rearrange("(k p) n -> p k n", p=128),
                          reads=[w_in], writes=[slot])
                    for half in range(2):
                        tab = tabs[half]
                        g0 = t0 + half * 512
                        for j in range(2):
                            hm = g2 * 2 + j
                            pb = gemm_fm(C, slot, wv, 128, j * 128, act_n, [nTp], 32, toks[half], psring=psD)
                            sq = square(C, pb, 128)
                            xg = scaled_copy(C, pb, 128, 15, pcs)

                            def st2(sq=sq, xg=xg, tab=tab, hm=hm, g0=g0):
                                rs = ss_rstd(C, cm, [(sq, 128)], DHD, psring=psD)
                                prot = rot_mm(C, cm, xg, 128, psring=psD)
                                t3 = rope_combine(C, xg, prot, 128, tab[:, 0, :], tab[:, 1, :], tab)
                                finish_mul(C, t3, 128, rs, sDK, sDK[hm, :, g0:g0 + 512])
                            dfd.push(st2)
                            hh = g4 * 2 + j
                            h = hg * 8 + hh
                            pbk = gemm_fm(C, uslot, uwv, 128, hh * 256, act_c, [ckvn], 4, toks[half], psring=psK)
                            sqk = square(C, pbk, 128)

                            def st2k(pbk=pbk, sqk=sqk, h=h, half=half, g0=g0):
                                rs = ss_rstd(C, cm, [(sqk, 128), (kpsq[half], 128)], QK, psring=psS)
                                finish_plain(C, pbk, 128, 12, pcs, rs, sKn, sKn[h, :, g0:g0 + 512])
                                finish_mul(C, kprot[half], 64, rs, sKr, sKr[h, :, g0:g0 + 512])
                            dfk.push(st2k)
                            for _ in range(2):
                                tl, pr = vgroups[vi]
                                vi += 1
                                if pr == 0:
                                    vstage[tl] = vst.next()
                                vs = vstage[tl]
                                pbv = psD.next()
                                for kc in range(4):
                                    rhs = uwv[:, kc, :].rearrange("p (h c) -> p h c", c=256)[:, 2 * pr:2 * pr + 2, 128:256]
                                    P.comp("pe", lambda e: e.matmul(
                                        pbv[:, 0:256].rearrange("p (h c) -> p h c", c=128),
                                        lhsT=ckvn[:, kc, tl * 128:(tl + 1) * 128], rhs=rhs,
                                        start=(kc == 0), stop=(kc == 3)), reads=[uslot, ckvn], writes=[pbv], signal=(kc == 3))
                                dstv = vs[:, pr * 256:(pr + 1) * 256]
                                if pr % 2 == 0:
                                    P.comp("act", lambda e: e.copy(out=dstv, in_=pbv[:, 0:256]),
                                           reads=[pbv], writes=[vs], nowaw=[vs])
                                else:
                                    P.comp("dve", lambda e: e.tensor_copy(out=dstv, in_=pbv[:, 0:256]),
                                           reads=[pbv], writes=[vs], nowaw=[vs])
                                if pr == 3:
                                    gt = p * 8 + tl
                                    P.dma("sp", sVm[hg * 8:(hg + 1) * 8, :, gt, :].rearrange("h p d -> p h d"),
                                          vs[:, 0:1024].rearrange("p (h d) -> p h d", h=8), reads=[vs], writes=[sVm], nowaw=[sVm])
                dfk.flush()
            dfd.flush()
            for g2 in range(8):
                slot, wv = load_w(C, w_in, 0, 32, C_DV + g2 * 256, 256)
                for tl in range(8):
                    gt = p * 8 + tl
                    pb = C["psr"].next()
                    for kc in range(32):
                        P.comp("pe", lambda e: e.matmul(
                            pb[:, 0:256], lhsT=nTp[:, kc, tl * 128:(tl + 1) * 128], rhs=wv[:, kc, :],
                            start=(kc == 0), stop=(kc == 31)), reads=[slot, nTp], writes=[pb], signal=(kc == 31))
                    ds = dvst.next()
                    if tl % 2 == 0:
                        P.comp("act", lambda e: e.copy(out=ds[:], in_=pb[:, 0:256]), reads=[pb], writes=[ds])
                    else:
                        P.comp("dve", lambda e: e.tensor_copy(out=ds[:], in_=pb[:, 0:256]), reads=[pb], writes=[ds])
                    P.dma("sp", sDV[g2, :, gt, :], ds[:], reads=[ds], writes=[sDV], nowaw=[sDV])
        ph.finish()

    def phase1b():
        ph = Phase()
        cm, pcs = load_consts(ph)
        C = make_p1_common(ph, cm, pcs)
        nTp = ph.sb("nTp", [128, 32, 1024], BF16)
        tab = ph.sb("tab", [128, 4, 1024], F32)
        cqraw = ph.sb("cqraw", [128, 6, 512], F32)
        cqsq = [ph.sb("cqsq", [128, 512], BF16) for _ in range(6)]
        cqn = ph.sb("cqn", [128, 6, 1024], BF16)
        P.dma("pool", nTp[:], nT[:, :, S:S + 1024].rearrange("k p t -> p k t"), reads=[nT], writes=[nTp])
        P.dma("pool", tab[:], tq[:], reads=[tq], writes=[tab])
        act_n = lambda kc, tok: nTp[:, kc, tok]
        for half in range(2):
            tok = slice(half * 512, (half + 1) * 512)
            g0 = half * 512
            for g2 in range(3):
                slot, wv = load_w(C, w_in, 0, 32, C_CQ + g2 * 256, 256)
                for j in range(2):
                    ch = g2 * 2 + j
                    pb = gemm_fm(C, slot, wv, 128, j * 128, act_n, [nTp], 32, tok)
                    P.comp("act", lambda e, pb=pb, ch=ch: e.activation(out=cqsq[ch][:], in_=pb[:], func=AF.Square),
                           reads=[pb], writes=[cqsq[ch]])
                    P.comp("act", lambda e, pb=pb, ch=ch: e.copy(out=cqraw[:, ch, :], in_=pb[:, :]),
                           reads=[pb], writes=[cqraw], nowaw=[cqraw])
            rs = ss_rstd(C, cm, [(cqsq[ch], 128) for ch in range(6)], QL)
            for ch in range(6):
                P.comp("dve", lambda e, ch=ch, rs=rs: e.scalar_tensor_tensor(
                    out=cqn[:, ch, tok], in0=cqraw[:, ch, :], scalar=pcs[:, ch:ch + 1], in1=rs[:],
                    op0=ALU.mult, op1=ALU.mult), reads=[cqraw, pcs, rs], writes=[cqn], nowaw=[cqn])
            act_q = lambda kc, tok: cqn[:, kc, tok]
            for hg in range(4):
                slot, wv = load_w(C, w_uq, 0, 6, hg * 768, 768)
                for hh in range(4):
                    h = hg * 4 + hh
                    pbn = gemm_fm(C, slot, wv, 128, hh * 192, act_q, [cqn], 6, tok)
                    pbr = gemm_fm(C, slot, wv, 64, hh * 192 + 128, act_q, [cqn], 6, tok)
                    sqn = square(C, pbn, 128)
                    sqr = square(C, pbr, 64)
                    xg = scaled_copy(C, pbr, 64, 11, pcs)
                    rs = ss_rstd(C, cm, [(sqn, 128), (sqr, 64)], QK)
                    prot = rot_mm(C, cm, xg, 64)
                    finish_plain(C, pbn, 128, 10, pcs, rs, sQn, sQn[h, :, g0:g0 + 512])
                    t3 = rope_combine(C, xg, prot, 64, tab[0:64, 2, tok], tab[0:64, 3, tok], tab)
                    finish_mul(C, t3, 64, rs, sQr, sQr[h, :, g0:g0 + 512])
            dfq = Deferred(1)
            for g2 in range(8):
                slot, wv = load_w(C, w_in, 0, 32, C_DQ + g2 * 256, 256)
                for j in range(2):
                    hm = g2 * 2 + j
                    pb = gemm_fm(C, slot, wv, 128, j * 128, act_n, [nTp], 32, tok)
                    sq = square(C, pb, 128)
                    xg = scaled_copy(C, pb, 128, 14, pcs)

                    def st2(sq=sq, xg=xg, hm=hm, g0=g0, tok=tok):
                        rs = ss_rstd(C, cm, [(sq, 128)], DHD)
                        prot = rot_mm(C, cm, xg, 128)
                        t3 = rope_combine(C, xg, prot, 128, tab[:, 0, tok], tab[:, 1, tok], tab)
                        finish_mul(C, t3, 128, rs, sDQ, sDQ[hm, :, g0:g0 + 512])
                    dfq.push(st2)
            dfq.flush()
        ph.finish()

    def phase2_3a():
        ph = Phase()
        cm, pcs = load_consts(ph)
        oT = ph.sb("oT", [128, 32, 1024], BF16)
        mk = ph.sb("mk", [128, 8, 128], BF16)
        P.dma("pool", mk[:], msk[:], reads=[msk], writes=[mk])
        lv = ph.sb("lv", [128, 512], F32)
        P.dma("pool", lv[:], lamv[:], reads=[lamv], writes=[lv])
        pr1 = ph.sb("pr1", [128, 256], F32)
        sm = ph.sb("sm", [128, 2], F32)
        ex = ph.sb("ex", [128, 2], F32)
        nlam = ph.sb("nlam", [128, 1], F32)
        gsub = ph.sb("gsub", [128, 2], F32)
        P.comp("dve", lambda e: e.tensor_tensor(out=pr1[:, 0:128], in0=lv[:, 0:128], in1=lv[:, 128:256], op=ALU.mult),
               reads=[lv], writes=[pr1])
        P.comp("dve", lambda e: e.tensor_tensor(out=pr1[:, 128:256], in0=lv[:, 256:384], in1=lv[:, 384:512], op=ALU.mult),
               reads=[lv, pr1], writes=[pr1])
        P.comp("dve", lambda e: e.reduce_sum(out=sm[:], in_=pr1[:].rearrange("p (a b) -> p a b", a=2), axis=AX.X),
               reads=[pr1], writes=[sm])
        P.comp("act", lambda e: e.activation(out=ex[:], in_=sm[:], func=AF.Exp), reads=[sm], writes=[ex])
        P.comp("dve", lambda e: e.scalar_tensor_tensor(out=nlam[:], in0=ex[:, 1:2], scalar=-LAMBDA_INIT, in1=ex[:, 0:1],
                                                       op0=ALU.add, op1=ALU.subtract), reads=[ex], writes=[nlam])
        P.comp("dve", lambda e: e.tensor_scalar(out=gsub[:], in0=pcs[:, 16:18], scalar1=(1.0 - LAMBDA_INIT), scalar2=None,
                                                op0=ALU.mult), reads=[pcs], writes=[gsub])
        NQ = 4
        pho = ph
        ph = Phase()
        kq = [ph.sb("kq", [128, S // NQ], BF16) for _ in range(NQ)]
        krq = [ph.sb("krq", [128, S // NQ], BF16) for _ in range(NQ)]
        vq = [ph.sb("vq", [128, 64 // NQ, 256], BF16) for _ in range(NQ)]
        qn = Ring([ph.sb("qn", [128, 1024], BF16) for _ in range(2)])
        qr = Ring([ph.sb("qr", [128, 1024], BF16) for _ in range(2)])
        for t_ in krq + qr.items:
            P.comp("dve", lambda e: e.memset(t_[64:128, :], 0.0), writes=[t_])
        ptr = Ring([ph.sb("pt", [128, 512], BF16) for _ in range(6)])
        sps = Ring([ph.ps("sps", [128, 512], F32) for _ in range(3)])
        lps1 = ph.ps("lps1", [128, 512], F32)
        acc = [ph.ps("acc", [128, 512], F32) for _ in range(4)]
        accL = [ph.sb("accL", [128, 512], F32) for _ in range(2)]
        lhi = Ring([ph.sb("lhi", [128, 512], BF16) for _ in range(2)])
        llo = Ring([ph.sb("llo", [128, 512], BF16) for _ in range(2)])
        rl = Ring([ph.sb("rl", [128, 512], F32) for _ in range(2)])
        t1s = ph.sb("t1s", [128, 2, 1024], F32)
        dsb = ph.sb("dsb", [128, 2, 512], F32)
        ub = Ring([ph.sb("ub", [128, 512], F32) for _ in range(2)])
        sq2 = [ph.sb("sq2", [128, 512], BF16) for _ in range(2)]
        rmb = ph.sb("rmb2", [128, 512], F32)
        rsb = ph.sb("rsb2", [128, 512], F32)
        KT = S // 128
        TPQ = KT // NQ

        def unit(kind, h, mp):
            nvc = 1 if kind == "mla" else 2
            vw = 128 * nvc
            scale = MLA_SCALE if kind == "mla" else DIFF_SCALE
            qnt = qn.next()
            if kind == "mla":
                qrt = qr.next()
                P.dma("pool", qnt[:], sQn[h], reads=[sQn], writes=[qnt])
                P.dma("pool", qrt[0:64, :], sQr[h], reads=[sQr], writes=[qrt], nowaw=[qrt])
            else:
                P.dma("pool", qnt[:], sDQ[2 * h + mp], reads=[sDQ], writes=[qnt])
            for q4 in range(NQ):
                ks = slice(q4 * (S // NQ), (q4 + 1) * (S // NQ))
                if kind == "mla":
                    P.dma("pool", kq[q4][:], sKn[h, :, ks], reads=[sKn], writes=[kq[q4]])
                    P.dma("pool", krq[q4][0:64, :], sKr[h, :, ks], reads=[sKr], writes=[krq[q4]], nowaw=[krq[q4]])
                    P.dma("pool", vq[q4][:, :, 0:128], sVm[h, :, q4 * TPQ:(q4 + 1) * TPQ, :], reads=[sVm], writes=[vq[q4]])
                else:
                    P.dma("pool", kq[q4][:], sDK[2 * h + mp, :, ks], reads=[sDK], writes=[kq[q4]])
                    P.dma("pool", vq[q4][:], sDV[h, :, q4 * TPQ:(q4 + 1) * TPQ, :], reads=[sDV], writes=[vq[q4]])
            plist = []
            for kt in range(KT):
                g = kt // 8
                for (c0, c1) in ([(128 * g, 512), (512, 1024)] if g < 4 else [(128 * g, 1024)]):
                    plist.append((kt, c0, c1))
            rope = kind == "mla"
            pts = {}

            def emit_s(i):
                kt, c0, c1 = plist[i]
                g, r = kt // 8, kt % 8
                q4, kl = kt // TPQ, kt % TPQ
                n = c1 - c0
                sp_ = sps.next()
                P.comp("pe", lambda e: e.matmul(
                    sp_[:, 0:n], lhsT=kq[q4][:, kl * 128:(kl + 1) * 128], rhs=qnt[:, c0:c1],
                    start=True, stop=(not rope)), reads=[kq[q4], qnt], writes=[sp_], signal=(not rope))
                if rope:
                    P.comp("pe", lambda e: e.matmul(
                        sp_[:, 0:n], lhsT=krq[q4][:, kl * 128:(kl + 1) * 128], rhs=qrt[:, c0:c1],
                        start=False, stop=True), reads=[krq[q4], qrt], writes=[sp_], signal=True)
                pt = ptr.next()
                P.comp("act", lambda e: e.activation(
                    out=pt[:, 0:n], in_=sp_[:, 0:n], func=AF.Exp, scale=scale), reads=[sp_], writes=[pt])
                if c0 == 128 * g:
                    P.comp("dve", lambda e: e.tensor_tensor(
                        out=pt[:, 0:128], in0=pt[:, 0:128], in1=mk[:, r, :], op=ALU.mult),
                        reads=[pt, mk], writes=[pt])
                b = 0 if c0 < 512 else 1
                off = c0 - 512 * b
                al = accL[b]
                if b == 1 and kt % 2 == 0:
                    pass
                elif kt == b:
                    P.comp("dve", lambda e: e.tensor_copy(out=al[:, off:off + n], in_=pt[:, 0:n]),
                           reads=[pt], writes=[al])
                else:
                    P.comp("dve", lambda e: e.tensor_tensor(
                        out=al[:, off:off + n], in0=al[:, off:off + n], in1=pt[:, 0:n], op=ALU.add),
                        reads=[pt, al], writes=[al])
                pts[i] = pt

            def emit_pv(i):
                kt, c0, c1 = plist[i]
                q4, kl = kt // TPQ, kt % TPQ
                n = c1 - c0
                b = 0 if c0 < 512 else 1
                off = c0 - 512 * b
                pt = pts.pop(i)
                first = kt == 0
                last = (kt == 31) if b == 0 else (kt == KT - 1)
                for vc in range(nvc):
                    a = acc[vc * 2 + b]
                    P.comp("pe", lambda e: e.matmul(
                        a[:, off:off + n], lhsT=vq[q4][:, kl, vc * 128:(vc + 1) * 128], rhs=pt[:, 0:n],
                        start=first, stop=last), reads=[vq[q4], pt], writes=[a], signal=(vc == nvc - 1))
                if b == 1 and kt % 2 == 0:
                    P.comp("pe", lambda e: e.matmul(
                        lps1[:, off:off + n], lhsT=cm[:, 1, :], rhs=pt[:, 0:n], start=first, stop=False),
                        reads=[cm, pt], writes=[lps1], signal=True)

            def epilogue(b):
                bs = slice(b * 512, (b + 1) * 512)
                rlt = rl.next()
                hi, lo = lhi.next(), llo.next()
                lps = lps1 if b == 1 else sps.next()
                P.comp("dve", lambda e: e.tensor_copy(out=hi[:], in_=accL[b][:]), reads=[accL[b]], writes=[hi])
                P.comp("dve", lambda e: e.tensor_tensor(out=lo[:], in0=accL[b][:], in1=hi[:], op=ALU.subtract),
                       reads=[accL[b], hi], writes=[lo])
                P.comp("pe", lambda e: e.matmul(lps[:, :], lhsT=cm[:, 1, :], rhs=hi[:], start=(b == 0), stop=False),
                       reads=[cm, hi], writes=[lps], signal=False)
                P.comp("pe", lambda e: e.matmul(lps[:, :], lhsT=cm[:, 1, :], rhs=lo[:], start=False, stop=True),
                       reads=[cm, lo], writes=[lps], signal=True)
                P.comp("dve", lambda e: e.reciprocal(out=rlt[:], in_=lps[:]), reads=[lps], writes=[rlt])
                if kind == "mla":
                    P.comp("dve", lambda e, rlt=rlt, b=b, bs=bs: e.tensor_tensor(
                        out=oT[:, h, bs], in0=acc[b][:], in1=rlt[:], op=ALU.mult),
                        reads=[acc[b], rlt], writes=[oT], nowaw=[oT])
                elif mp == 0:
                    for vc in range(2):
                        P.comp("dve", lambda e, rlt=rlt, b=b, bs=bs, vc=vc: e.tensor_tensor(
                            out=t1s[:, vc, bs], in0=acc[vc * 2 + b][:], in1=rlt[:], op=ALU.mult),
                            reads=[acc[vc * 2 + b], rlt], writes=[t1s], nowaw=[t1s])
                else:
                    for vc in range(2):
                        u = ub.next()
                        P.comp("dve", lambda e, rlt=rlt, b=b, vc=vc, u=u: e.tensor_tensor(
                            out=u[:], in0=acc[vc * 2 + b][:], in1=rlt[:], op=ALU.mult),
                            reads=[acc[vc * 2 + b], rlt], writes=[u])
                        P.comp("dve", lambda e, u=u, vc=vc, bs=bs: e.scalar_tensor_tensor(
                            out=dsb[:, vc, :], in0=u[:], scalar=nlam[:, 0:1], in1=t1s[:, vc, bs],
                            op0=ALU.mult, op1=ALU.add), reads=[u, nlam, t1s], writes=[dsb], nowaw=[dsb])
                        P.comp("act", lambda e, vc=vc: e.activation(out=sq2[vc][:], in_=dsb[:, vc, :], func=AF.Square),
                               reads=[dsb], writes=[sq2[vc]])
                    sp_ = sps.next()
                    for vc in range(2):
                        P.comp("pe", lambda e, sp_=sp_, vc=vc: e.matmul(
                            sp_[:, :], lhsT=cm[:, 1, :], rhs=sq2[vc][:], start=(vc == 0), stop=(vc == 1)),
                            reads=[cm, sq2[vc]], writes=[sp_], signal=(vc == 1))
                    P.comp("act", lambda e, sp_=sp_: e.activation(out=rmb[:], in_=sp_[:], func=AF.Sqrt,
                                                                  scale=1.0 / DV, bias=EPS), reads=[sp_], writes=[rmb])
                    P.comp("dve", lambda e: e.reciprocal(out=rsb[:], in_=rmb[:]), reads=[rmb], writes=[rsb])
                    for vc in range(2):
                        P.comp("dve", lambda e, vc=vc, bs=bs: e.scalar_tensor_tensor(
                            out=oT[:, 16 + 2 * h + vc, bs], in0=dsb[:, vc, :], scalar=gsub[:, vc:vc + 1], in1=rsb[:],
                            op0=ALU.mult, op1=ALU.mult), reads=[dsb, gsub, rsb], writes=[oT], nowaw=[oT])

            LOOK = 2
            for i in range(min(LOOK, len(plist))):
                emit_s(i)
            for i in range(len(plist)):
                if i + LOOK < len(plist):
                    emit_s(i + LOOK)
                emit_pv(i)
                if plist[i][0] == 31 and plist[i][1] < 512:
                    epilogue(0)
            epilogue(1)

        for h in range(NH):
            unit("mla", h, 0)
        for h in range(DH):
            unit("diff", h, 0)
            unit("diff", h, 1)

        ph.finish(extra=pho.ts)
        ph = Phase()
        ring = Ring([ph.sb("wr3", [128, 32, 512], BF16) for _ in range(2)])
        psw = Ring([ph.ps("psw", [128, 512], F32) for _ in range(8)])
        xs = Ring([ph.sb("xs", [128, 512], F32) for _ in range(4)])
        hs = Ring([ph.sb("hs", [128, 512], F32) for _ in range(4)])
        for gcol in range(D // 512):
            slot = ring.next()
            cs = slice(gcol * 512, (gcol + 1) * 512)
            for hk in range(2):
                P.dma("pool", slot[:, hk * 16:(hk + 1) * 16, :],
                      w_o[hk * 2048:(hk + 1) * 2048, cs].rearrange("(k p) n -> p k n", p=128),
                      reads=[w_o], writes=[slot], nowaw=[slot])
            for tl in range(8):
                xst = xs.next()
                P.dma("sp", xst[:], xo[tl * 128:(tl + 1) * 128, cs], reads=[xo], writes=[xst])
                pb = psw.next()
                for kc in range(32):
                    P.comp("pe", lambda e: e.matmul(
                        pb[:, :], lhsT=oT[:, kc, tl * 128:(tl + 1) * 128], rhs=slot[:, kc, :],
                        start=(kc == 0), stop=(kc == 31)), reads=[oT, slot], writes=[pb], signal=(kc == 31))
                hst = hs.next()
                P.comp("dve", lambda e: e.tensor_tensor(
                    out=hst[:], in0=pb[:, :], in1=xst[:], op=ALU.add), reads=[pb, xst], writes=[hst])
                P.dma("sp", hbuf[tl * 128:(tl + 1) * 128, cs], hst[:],
                      reads=[hst], writes=[hbuf], nowaw=[hbuf])
        ph.finish(extra=pho.ts)
        pho.es.close()

    def phase3c():
        ph = Phase()
        mT = ph.sb("mT", [128, 32, 1024], BF16)
        P.dma("pool", mT[:], nT[:, :, S:S + 1024].rearrange("k p t -> p k t"), reads=[nT], writes=[mT])
        actT = ph.sb("actT", [128, 16, 1024], BF16)
        ring = Ring([ph.sb("wr4", [128, 8192], BF16) for _ in range(4)])
        psr = Ring([ph.ps("psf", [128, 512], F32) for _ in range(8)])
        sg = Ring([ph.sb("sg", [128, 512], F32) for _ in range(3)])
        bs_ = Ring([ph.sb("bs", [128, 512], F32) for _ in range(6)])
        rs_ = Ring([ph.sb("rs", [128, 512], F32) for _ in range(4)])
        NCH = DFF // 128
        rounds = [(0, 16), (16, 16), (32, 16), (48, 16), (64, 16), (80, 6)]
        for ri, (f0, nf) in enumerate(rounds):
            for g2 in range(nf // 2):
                c0 = (f0 + g2 * 2) * 128
                sg_, wg = ring.next(), None
                wg = sg_[:, 0:8192].rearrange("p (k n) -> p k n", k=32)
                P.dma("pool", wg, w_gate[:, c0:c0 + 256].rearrange("(k p) n -> p k n", p=128), reads=[w_gate], writes=[sg_])
                su_ = ring.next()
                wu = su_[:, 0:8192].rearrange("p (k n) -> p k n", k=32)
                P.dma("pool", wu, w_up[:, c0:c0 + 256].rearrange("(k p) n -> p k n", p=128), reads=[w_up], writes=[su_])
                for j in range(2):
                    fl = g2 * 2 + j
                    for half in range(2):
                        tok = slice(half * 512, (half + 1) * 512)
                        pg = psr.next()
                        for kc in range(32):
                            P.comp("pe", lambda e, pg=pg, kc=kc, j=j, tok=tok, wg=wg: e.matmul(
                                pg[:, :], lhsT=wg[:, kc, j * 128:(j + 1) * 128], rhs=mT[:, kc, tok],
                                start=(kc == 0), stop=(kc == 31)), reads=[sg_, mT], writes=[pg], signal=(kc == 31))
                        pu = psr.next()
                        for kc in range(32):
                            P.comp("pe", lambda e, pu=pu, kc=kc, j=j, tok=tok, wu=wu: e.matmul(
                                pu[:, :], lhsT=wu[:, kc, j * 128:(j + 1) * 128], rhs=mT[:, kc, tok],
                                start=(kc == 0), stop=(kc == 31)), reads=[su_, mT], writes=[pu], signal=(kc == 31))
                        sgt = sg.next()
                        P.comp("act", lambda e, pg=pg, sgt=sgt: e.activation(out=sgt[:], in_=pg[:], func=AF.Silu),
                               reads=[pg], writes=[sgt])
                        P.comp("dve", lambda e, pu=pu, sgt=sgt, fl=fl, tok=tok: e.tensor_tensor(
                            out=actT[:, fl, tok], in0=pu[:], in1=sgt[:], op=ALU.mult),
                            reads=[pu, sgt], writes=[actT], nowaw=[actT])
            srcb = hbuf if ri == 0 else out
            items = [(gcol, tl) for gcol in range(D // 512) for tl in range(8)]
            bases = {}
            nb = [0]

            def emit_base(upto):
                while nb[0] < min(upto, len(items)):
                    gcol, tl = items[nb[0]]
                    base = bs_.next()
                    P.dma("sp", base[:], srcb[tl * 128:(tl + 1) * 128, gcol * 512:(gcol + 1) * 512],
                          reads=[srcb], writes=[base])
                    bases[nb[0]] = base
                    nb[0] += 1

            for ii, (gcol, tl) in enumerate(items):
                if tl == 0:
                    slot = ring.next()
                    wd = slot[:, 0:nf * 512].rearrange("p (k n) -> p k n", k=nf)
                    P.dma("pool", wd, w_down[f0 * 128:(f0 + nf) * 128, gcol * 512:(gcol + 1) * 512].rearrange(
                        "(k p) n -> p k n", p=128), reads=[w_down], writes=[slot])
                emit_base(ii + 3)
                base = bases.pop(ii)
                pb = psr.next()
                for f in range(nf):
                    P.comp("pe", lambda e: e.matmul(
                        pb[:, :], lhsT=actT[:, f, tl * 128:(tl + 1) * 128], rhs=wd[:, f, :],
                        start=(f == 0), stop=(f == nf - 1)), reads=[actT, slot], writes=[pb], signal=(f == nf - 1))
                res = rs_.next()
                P.comp("dve", lambda e: e.tensor_tensor(
                    out=res[:], in0=pb[:, :], in1=base[:], op=ALU.add), reads=[pb, base], writes=[res])
                P.dma("sp", out[tl * 128:(tl + 1) * 128, gcol * 512:(gcol + 1) * 512], res[:],
                      reads=[res], writes=[out], nowaw=[out])
        ph.finish()

    norm_rows(x, S, 0, 0)
    norm_rows(xo, TOWN, 0, S)
    phase1a()
    phase1b()
    phase2_3a()
    norm_rows(hbuf, TOWN, 1, S)
    phase3c()
    return nc


DEBUG_OUT = set()

_CACHE = {}


def _rope_tab(pos, dim):
    half = dim // 2
    inv = 10000.0 ** (-np.arange(half, dtype=np.float64) / half)
    ang = pos.astype(np.float64)[None, :] * np.concatenate([inv, inv])[:, None]
    return np.cos(ang).astype(np.float32), np.sin(ang).astype(np.float32)


def _rot_lhsT(dim):
    h = dim // 2
    m = np.zeros((128, 128), np.float32)
    for mm in range(dim):
        if mm < h:
            m[mm + h, mm] = -1.0
        else:
            m[mm - h, mm] = 1.0
    return m


def kernel(**inputs):
    f = lambda k: np.ascontiguousarray(np.asarray(inputs[k], dtype=np.float32))
    x = f("x")[0]
    bf = ml_dtypes.bfloat16
    pc = np.zeros((128, 32), np.float32)
    pc[:, 0:6] = f("q_latent_norm_g")[0].reshape(6, 128).T
    pc[:, 6:10] = f("kv_latent_norm_g")[0].reshape(4, 128).T
    gq, gk = f("mla_q_norm_g")[0], f("mla_k_norm_g")[0]
    pc[:, 10] = gq[:128]
    pc[:64, 11] = gq[128:]
    pc[:, 12] = gk[:128]
    pc[:64, 13] = gk[128:]
    pc[:, 14] = f("diff_q_norm_g")[0]
    pc[:, 15] = f("diff_k_norm_g")[0]
    pc[:, 16:18] = f("diff_subln_g")[0].reshape(2, 128).T
    gb = np.stack([np.broadcast_to(f("attn_norm_g")[0], (128, D)), np.broadcast_to(f("ffn_norm_g")[0], (128, D))])
    gb = np.ascontiguousarray(gb)
    lamv = np.concatenate([f("lambda_q1")[0], f("lambda_k1")[0], f("lambda_q2")[0], f("lambda_k2")[0]])
    lamv = np.ascontiguousarray(np.broadcast_to(lamv, (128, 512)))
    cmat = np.zeros((4, 128, 128), np.float32)
    cmat[0] = np.eye(128)
    cmat[1] = 1.0
    cmat[2] = _rot_lhsT(128)
    cmat[3] = _rot_lhsT(64)
    cmat = cmat.astype(bf)
    pos = np.arange(S)
    c128, s128 = _rope_tab(pos, 128)
    c64, s64 = _rope_tab(pos, 64)
    tk = np.zeros((128, 4, S), np.float32)
    tk[:, 0], tk[:, 1] = c128, s128
    tk[:64, 2], tk[:64, 3] = c64, s64
    shared = {
        "x": x, "w_in": f("w_in")[0], "w_uq": f("w_uq")[0], "w_ukv": f("w_ukv")[0], "w_o": f("w_o")[0],
        "w_gate": f("w_gate")[0], "w_up": f("w_up")[0], "w_down": f("w_down")[0],
        "gb": gb, "pc": pc, "lamv": lamv, "cmat": cmat, "tk": tk,
    }
    in_maps = []
    own_idx = []
    kk = np.arange(128)
    for c in range(NCORES):
        idx = np.concatenate([np.arange((8 * j + c) * 128, (8 * j + c + 1) * 128) for j in range(8)])
        own_idx.append(idx)
        msk = np.zeros((128, 8, 128), np.float32)
        for r in range(8):
            if r < c:
                msk[:, r, :] = 1.0
            elif r == c:
                msk[:, r, :] = ((kk[:, None] // 64) <= (kk[None, :] // 64)).astype(np.float32)
        m = dict(shared)
        m["xo"] = np.ascontiguousarray(x[idx])
        m["tq"] = np.ascontiguousarray(tk[:, :, idx])
        m["msk"] = msk.astype(bf)
        in_maps.append(m)
    if "nc" not in _CACHE:
        _CACHE["nc"] = build_program()
    res = run_bass_kernel_spmd(_CACHE["nc"], in_maps, core_ids=list(range(NCORES)))
    _CACHE["res"] = res
    outp = np.empty((S, D), np.float32)
    for c in range(NCORES):
        outp[own_idx[c]] = np.asarray(res.results[c]["out"], dtype=np.float32)
    return outp[None]
```
